# Optimizing a Trainium2 kernel written in Bass

```python
import jax
import jax.numpy as jnp
from jax import lax
import numpy as np

D_MODEL = 1024
BATCH = 4
SEQ = 4096
DEPTH = 4

CTX_LEN = 256
GRID_W = 64
N_MIXERS = 3
HEAD_DIM = 64
N_HEADS = D_MODEL // HEAD_DIM
N_KV_HEADS = max(1, N_HEADS // 4)
NA_WIN_H = 8
NA_WIN_W = 16
NA_QCOLS = 16
NA_KCOLS = NA_QCOLS + NA_WIN_W
SWA_RADIUS = 128
Q_BLOCK = 128
ROPE_THETA = 10000.0
D_FF = ((8 * D_MODEL // 3 + 127) // 128) * 128
CONV_W = 3
RMS_EPS = 1e-6
NEG_INF = -1e30
N_LAYERS_NA = (DEPTH + N_MIXERS - 1) // N_MIXERS
N_LAYERS_SWA = (DEPTH + N_MIXERS - 2) // N_MIXERS
N_LAYERS_GA = (DEPTH + N_MIXERS - 3) // N_MIXERS

kernel_name = 'hybrid_na_swa_axial_gqa_convffn_trunk'


def rmsnorm(x, g):
    x32 = x.astype(jnp.float32)
    y = x32 * lax.rsqrt(jnp.mean(x32 * x32, axis=-1, keepdims=True) + RMS_EPS)
    return (y * g.astype(jnp.float32)).astype(x.dtype)


def modulate(h, shift, scale):
    return h * (1 + scale) + shift


def axial_rope_tables(seq):
    t = jnp.arange(seq)
    row = (t // GRID_W).astype(jnp.float32)
    col = (t % GRID_W).astype(jnp.float32)
    n_axis = HEAD_DIM // 4
    inv = ROPE_THETA ** (-jnp.arange(n_axis, dtype=jnp.float32) / n_axis)
    ang = jnp.concatenate([row[:, None] * inv, col[:, None] * inv], axis=-1)
    return jnp.cos(ang), jnp.sin(ang)


def apply_rope(x, cos, sin):
    shp = (x.shape[1],) + (1,) * (x.ndim - 3) + (x.shape[-1] // 2,)
    cos = cos.reshape(shp).astype(x.dtype)
    sin = sin.reshape(shp).astype(x.dtype)
    x1, x2 = jnp.split(x, 2, axis=-1)
    return jnp.concatenate([x1 * cos - x2 * sin, x2 * cos + x1 * sin], axis=-1)


def attend(q, k, v, bias=None, mask=None, sink=None):
    s = jnp.einsum('...qngd,...knd->...ngqk', q, k).astype(jnp.float32) * (HEAD_DIM ** -0.5)
    if bias is not None:
        s = s + bias.astype(jnp.float32)
    if mask is not None:
        s = jnp.where(mask, s, NEG_INF)
    if sink is not None:
        sink_col = jnp.broadcast_to(sink.astype(jnp.float32)[:, :, None, None], s.shape[:-1] + (1,))
        s = jnp.concatenate([s, sink_col], axis=-1)
    p = jax.nn.softmax(s, axis=-1)
    if sink is not None:
        p = p[..., :-1]
    return jnp.einsum('...ngqk,...knd->...qngd', p.astype(v.dtype), v)


def attn_inputs(h, w_qkv, g_q, g_k, n_kv, with_q=True):
    bsz, t = h.shape[:2]
    kv_w = n_kv * HEAD_DIM
    if with_q:
        y = h @ w_qkv
        q = rmsnorm(y[..., :D_MODEL].reshape(bsz, t, n_kv, N_HEADS // n_kv, HEAD_DIM), g_q)
        kv = y[..., D_MODEL:]
    else:
        q = None
        kv = h @ w_qkv[:, D_MODEL:]
    k = rmsnorm(kv[..., :kv_w].reshape(bsz, t, n_kv, HEAD_DIM), g_k)
    v = kv[..., kv_w:].reshape(bsz, t, n_kv, HEAD_DIM)
    return q, k, v


def neighbourhood_attention(q, k, v, k_c, v_c, rpb):
    bsz, seq = q.shape[:2]
    n_ctx = k_c.shape[1]
    rows = seq // GRID_W
    kh = min(NA_WIN_H, rows)
    ncb = GRID_W // NA_QCOLS
    qg = q.reshape(bsz, rows, ncb, NA_QCOLS, N_HEADS, 1, HEAD_DIM)
    kg = k.reshape(bsz, rows, GRID_W, N_HEADS, HEAD_DIM)
    vg = v.reshape(bsz, rows, GRID_W, N_HEADS, HEAD_DIM)
    qcol = np.arange(GRID_W).reshape(ncb, NA_QCOLS)
    kcs = np.clip(np.arange(ncb) * NA_QCOLS - NA_WIN_W // 2, 0, GRID_W - NA_KCOLS)
    kcol = kcs[:, None] + np.arange(NA_KCOLS)
    cs = np.clip(qcol - NA_WIN_W // 2, 0, GRID_W - NA_WIN_W)
    col_ok = (kcol[:, None, :] >= cs[..., None]) & (kcol[:, None, :] < cs[..., None] + NA_WIN_W)
    dc_idx = np.clip(kcol[:, None, :] - qcol[..., None], -(NA_WIN_W - 1), NA_WIN_W - 1) + NA_WIN_W - 1
    lat_ok = np.broadcast_to(col_ok[:, :, None, :], (ncb, NA_QCOLS, kh, NA_KCOLS)).reshape(ncb, NA_QCOLS, kh * NA_KCOLS)
    mask = np.concatenate([lat_ok, np.ones((ncb, NA_QCOLS, n_ctx), bool)], axis=-1)[:, None, None]
    k_cb = jnp.broadcast_to(k_c[:, None], (bsz, ncb) + k_c.shape[1:])
    v_cb = jnp.broadcast_to(v_c[:, None], (bsz, ncb) + v_c.shape[1:])
    zero_ctx_bias = jnp.zeros((ncb, N_HEADS, NA_QCOLS, n_ctx), rpb.dtype)

    def row_step(r):
        rs = jnp.clip(r - kh // 2, 0, rows - kh)
        k_r = lax.dynamic_slice_in_dim(kg, rs, kh, axis=1)
        v_r = lax.dynamic_slice_in_dim(vg, rs, kh, axis=1)
        k_b = jnp.moveaxis(k_r[:, :, kcol], 1, 2).reshape(bsz, ncb, kh * NA_KCOLS, N_HEADS, HEAD_DIM)
        v_b = jnp.moveaxis(v_r[:, :, kcol], 1, 2).reshape(bsz, ncb, kh * NA_KCOLS, N_HEADS, HEAD_DIM)
        k_cat = jnp.concatenate([k_b, k_cb], axis=2)
        v_cat = jnp.concatenate([v_b, v_cb], axis=2)
        dr_idx = rs + jnp.arange(kh) - r + NA_WIN_H - 1
        b_lat = rpb[:, dr_idx[:, None, None, None], dc_idx[None]]
        b_lat = jnp.transpose(b_lat, (2, 0, 3, 1, 4)).reshape(ncb, N_HEADS, NA_QCOLS, kh * NA_KCOLS)
        bias = jnp.concatenate([b_lat, zero_ctx_bias], axis=-1)[:, :, None]
        q_r = lax.dynamic_index_in_dim(qg, r, axis=1, keepdims=False)
        return attend(q_r, k_cat, v_cat, bias=bias, mask=mask)

    out = lax.map(row_step, jnp.arange(rows))
    return jnp.moveaxis(out, 0, 1).reshape(bsz, seq, D_MODEL)


def sliding_window_attention(q, k, v, k_c, v_c, sink):
    bsz, seq = q.shape[:2]
    n_ctx = k_c.shape[1]
    nb = seq // Q_BLOCK
    span = Q_BLOCK + 2 * SWA_RADIUS
    pad = ((0, 0), (SWA_RADIUS, SWA_RADIUS), (0, 0), (0, 0))
    k_p = jnp.pad(k, pad)
    v_p = jnp.pad(v, pad)
    qoff = np.arange(Q_BLOCK)
    koff = np.arange(span) - SWA_RADIUS
    band = np.abs(qoff[:, None] - koff[None, :]) <= SWA_RADIUS
    ctx_ok = jnp.ones((Q_BLOCK, n_ctx), bool)

    def block_step(j):
        start = j * Q_BLOCK
        q_j = lax.dynamic_slice_in_dim(q, start, Q_BLOCK, axis=1)
        k_j = lax.dynamic_slice_in_dim(k_p, start, span, axis=1)
        v_j = lax.dynamic_slice_in_dim(v_p, start, span, axis=1)
        kpos = start - SWA_RADIUS + jnp.arange(span)
        valid = band & ((kpos >= 0) & (kpos < seq))[None, :]
        mask = jnp.concatenate([valid, ctx_ok], axis=-1)
        return attend(q_j, jnp.concatenate([k_j, k_c], axis=1), jnp.concatenate([v_j, v_c], axis=1),
                      mask=mask, sink=sink)

    out = lax.map(block_step, jnp.arange(nb))
    return jnp.moveaxis(out, 0, 1).reshape(bsz, seq, D_MODEL)


def blocked_global_attention(q, k, v, k_c, v_c):
    bsz, seq = q.shape[:2]
    nb = seq // Q_BLOCK
    k_all = jnp.concatenate([k, k_c], axis=1)
    v_all = jnp.concatenate([v, v_c], axis=1)

    def block_step(j):
        q_j = lax.dynamic_slice_in_dim(q, j * Q_BLOCK, Q_BLOCK, axis=1)
        return attend(q_j, k_all, v_all)

    out = lax.map(block_step, jnp.arange(nb))
    return jnp.moveaxis(out, 0, 1).reshape(bsz, seq, D_MODEL)


def conv_ffn(h, w_up, conv_w, conv_b, w_down):
    t = h.shape[1]
    u = h @ w_up
    u_p = jnp.pad(u, ((0, 0), (CONV_W // 2, CONV_W // 2), (0, 0)))
    u = sum(u_p[:, tap:tap + t] * conv_w[tap] for tap in range(CONV_W)) + conv_b
    a, g = jnp.split(u, 2, axis=-1)
    return (a * jax.nn.silu(g)) @ w_down


def setup_inputs(seed: int = 0) -> dict:
    key = jax.random.key(seed)
    ks = iter(jax.random.split(key, 32))
    d = D_MODEL
    qkv_na = d + 2 * N_HEADS * HEAD_DIM
    qkv_gqa = d + 2 * N_KV_HEADS * HEAD_DIM

    def nrm(shape, scale):
        return scale * jax.random.normal(next(ks), shape, jnp.float32)

    return {
        'x': nrm((BATCH, SEQ, d), 1.0),
        'c': nrm((BATCH, d), 1.0),
        'ctx': nrm((BATCH, CTX_LEN, d), 1.0),
        'c_ctx': nrm((d,), 1.0),
        'w_mod': nrm((DEPTH, d, 6 * d), 0.5 * d ** -0.5),
        'b_mod': nrm((DEPTH, 6 * d), 0.02),
        'g_attn': 1.0 + nrm((DEPTH, d), 0.02),
        'g_ffn': 1.0 + nrm((DEPTH, d), 0.02),
        'na_w_qkv': nrm((N_LAYERS_NA, d, qkv_na), d ** -0.5),
        'na_g_q': 1.0 + nrm((N_LAYERS_NA, HEAD_DIM), 0.02),
        'na_g_k': 1.0 + nrm((N_LAYERS_NA, HEAD_DIM), 0.02),
        'na_rpb': nrm((N_LAYERS_NA, N_HEADS, 2 * NA_WIN_H - 1, 2 * NA_WIN_W - 1), 0.5),
        'na_w_o': nrm((N_LAYERS_NA, d, d), d ** -0.5),
        'swa_w_qkv': nrm((N_LAYERS_SWA, d, qkv_gqa), d ** -0.5),
        'swa_g_q': 1.0 + nrm((N_LAYERS_SWA, HEAD_DIM), 0.02),
        'swa_g_k': 1.0 + nrm((N_LAYERS_SWA, HEAD_DIM), 0.02),
        'swa_sink': nrm((N_LAYERS_SWA, N_HEADS), 0.5),
        'swa_w_o': nrm((N_LAYERS_SWA, d, d), d ** -0.5),
        'ga_w_qkv': nrm((N_LAYERS_GA, d, qkv_gqa), d ** -0.5),
        'ga_g_q': 1.0 + nrm((N_LAYERS_GA, HEAD_DIM), 0.02),
        'ga_g_k': 1.0 + nrm((N_LAYERS_GA, HEAD_DIM), 0.02),
        'ga_w_o': nrm((N_LAYERS_GA, d, d), d ** -0.5),
        'ffn_w_up': nrm((DEPTH, d, 2 * D_FF), d ** -0.5),
        'ffn_conv_w': nrm((DEPTH, CONV_W, 2 * D_FF), CONV_W ** -0.5),
        'ffn_conv_b': nrm((DEPTH, 2 * D_FF), 0.02),
        'ffn_w_down': nrm((DEPTH, D_FF, d), D_FF ** -0.5),
    }


def reference(x, c, ctx, c_ctx, w_mod, b_mod, g_attn, g_ffn,
              na_w_qkv, na_g_q, na_g_k, na_rpb, na_w_o,
              swa_w_qkv, swa_g_q, swa_g_k, swa_sink, swa_w_o,
              ga_w_qkv, ga_g_q, ga_g_k, ga_w_o,
              ffn_w_up, ffn_conv_w, ffn_conv_b, ffn_w_down):
    bsz, seq = x.shape[:2]
    n_ctx = ctx.shape[1]
    cos, sin = axial_rope_tables(seq)
    silu_c = jax.nn.silu(c)
    silu_cc = jax.nn.silu(c_ctx)
    for i in range(DEPTH):
        last = i == DEPTH - 1
        kind, j = i % N_MIXERS, i // N_MIXERS
        mod = silu_c @ w_mod[i] + b_mod[i]
        mod_c = silu_cc @ w_mod[i] + b_mod[i]
        sh1, sc1, gt1, sh2, sc2, gt2 = jnp.split(mod[:, None, :], 6, axis=-1)
        sh1c, sc1c, gt1c, sh2c, sc2c, gt2c = jnp.split(mod_c, 6, axis=-1)

        h = modulate(rmsnorm(x, g_attn[i]), sh1, sc1)
        hc = modulate(rmsnorm(ctx, g_attn[i]), sh1c, sc1c)
        if kind == 0:
            w_in, w_out, gq, gk, n_kv, sink = na_w_qkv[j], na_w_o[j], na_g_q[j], na_g_k[j], N_HEADS, None
        elif kind == 1:
            w_in, w_out, gq, gk, n_kv = swa_w_qkv[j], swa_w_o[j], swa_g_q[j], swa_g_k[j], N_KV_HEADS
            sink = swa_sink[j].reshape(n_kv, N_HEADS // n_kv)
        else:
            w_in, w_out, gq, gk, n_kv, sink = ga_w_qkv[j], ga_w_o[j], ga_g_q[j], ga_g_k[j], N_KV_HEADS, None
        q, k, v = attn_inputs(h, w_in, gq, gk, n_kv)
        q_c, k_c, v_c = attn_inputs(hc, w_in, gq, gk, n_kv, with_q=not last)
        if kind == 0:
            o = neighbourhood_attention(q, k, v, k_c, v_c, na_rpb[j])
        else:
            q = apply_rope(q, cos, sin)
            k = apply_rope(k, cos, sin)
            if kind == 1:
                o = sliding_window_attention(q, k, v, k_c, v_c, sink)
            else:
                o = blocked_global_attention(q, k, v, k_c, v_c)
        x = x + gt1 * (o @ w_out)
        if not last:
            o_c = attend(q_c, k_c, v_c, sink=sink).reshape(bsz, n_ctx, D_MODEL)
            ctx = ctx + gt1c * (o_c @ w_out)

        h = modulate(rmsnorm(x, g_ffn[i]), sh2, sc2)
        x = x + gt2 * conv_ffn(h, ffn_w_up[i], ffn_conv_w[i], ffn_conv_b[i], ffn_w_down[i])
        if not last:
            hc = modulate(rmsnorm(ctx, g_ffn[i]), sh2c, sc2c)
            ctx = ctx + gt2c * conv_ffn(hc, ffn_w_up[i], ffn_conv_w[i], ffn_conv_b[i], ffn_w_down[i])
    return x
```

```python
import numpy as np
from collections import deque
from contextlib import ExitStack
import concourse.bass as bass
import concourse.mybir as mybir
from concourse.bass_utils import run_bass_kernel_spmd

F32 = mybir.dt.float32
BF16 = mybir.dt.bfloat16
AF = mybir.ActivationFunctionType
ALU = mybir.AluOpType

D = 1024
NTOK = 2304
NLAT = 2048
NCTX = 256
DFF = 2816
NF = 22
NEG = -30000.0
NEGM = -240000.0
EPS = 1e-6
PAIRS = [[0, 1], [2, 3], [4, 5], [6, 7]]
ARENA_F32 = 33536
SAME_ENGINE_WAITS = True


def _prod(s):
    r = 1
    for a in s:
        r *= a
    return r


class V:
    __slots__ = ("ap", "k")

    def __init__(self, ap, k):
        self.ap = ap
        self.k = k


class Buf:
    def __init__(self, tens, off32, shape, dtype, pref, cell=2048):
        self.esz = 2 if dtype == BF16 else 4
        self.shape = list(shape)
        n = _prod(shape)
        n32 = (n * self.esz + 3) // 4
        self.off_b = off32 * 4
        self.n32 = n32
        ap = tens[:, off32:off32 + n32]
        if dtype == BF16:
            ap = ap.bitcast(BF16)[:, 0:n]
        if len(shape) == 2:
            ap = ap.rearrange("p (a b) -> p a b", a=shape[0])
        elif len(shape) == 3:
            ap = ap.rearrange("p (a b c) -> p a b c", a=shape[0], b=shape[1])
        self.ap = ap
        self.pref = pref
        self.cell = cell
        st = []
        acc = 1
        for s in reversed(shape):
            st.append(acc)
            acc *= s
        self.strides = list(reversed(st))

    def __getitem__(self, idx):
        if not isinstance(idx, tuple):
            idx = (idx,)
        sub = self.ap[idx]
        lo = 0
        hi = 0
        fidx = list(idx[1:]) + [slice(None)] * (len(self.shape) - len(idx) + 1)
        for d, ix in enumerate(fidx):
            if isinstance(ix, slice):
                a = 0 if ix.start is None else ix.start
                b = self.shape[d] if ix.stop is None else ix.stop
            else:
                a, b = ix, ix + 1
            lo += a * self.strides[d]
            hi += (b - 1) * self.strides[d]
        c0 = (self.off_b + lo * self.esz) // self.cell
        c1 = (self.off_b + hi * self.esz) // self.cell
        return V(sub, [(self.pref, c) for c in range(c0, c1 + 1)])

    def all(self):
        return self[(slice(None),)]


class Prog:
    ENG = ("pe", "act", "dve", "pool", "sp")

    def __init__(self, nc, es):
        self.nc = nc
        self.es = es
        self.ops = {e: [] for e in self.ENG}
        self.cnt = {e: 0 for e in self.ENG}
        self.sem = {e: es.enter_context(nc.semaphore("s_" + e)) for e in self.ENG}
        self.seen = {e: {} for e in self.ENG}
        self.lastw = {}
        self.readers = {}
        self.dsem = {}
        self.nwaits = 0

    def _deps(self, reads, writes):
        d = []
        for k in reads:
            w = self.lastw.get(k)
            if w is not None:
                d.append(w)
        for k in writes:
            w = self.lastw.get(k)
            if w is not None:
                d.append(w)
            d.extend(self.readers.get(k, ()))
        return d

    def _waits(self, eng, deps):
        best = {}
        for (sname, sem, val) in deps:
            if sname == "s_" + eng and (eng == "pe" or not SAME_ENGINE_WAITS):
                continue
            if val > best.get(sname, (None, 0))[1]:
                best[sname] = (sem, val)
        for sname, (sem, val) in best.items():
            if self.seen[eng].get(sname, 0) >= val:
                continue
            self.seen[eng][sname] = val
            self.nwaits += 1
            self.ops[eng].append(lambda e, sem=sem, val=val: e.wait_ge(sem, val))

    def _record(self, ev, reads, writes):
        for k in reads:
            self.readers.setdefault(k, []).append(ev)
        for k in writes:
            self.lastw[k] = ev
            self.readers[k] = []

    def op(self, eng, fn, reads=(), writes=()):
        self._waits(eng, self._deps(reads, writes))
        self.cnt[eng] += 1
        sem = self.sem[eng]
        self.ops[eng].append(lambda e, fn=fn, sem=sem: fn(e).then_inc(sem, 1))
        self._record(("s_" + eng, sem, self.cnt[eng]), reads, writes)

    def dma(self, out, in_, semname, eng="sp"):
        reads, writes = in_.k, out.k
        self._waits(eng, self._deps(reads, writes))
        if semname not in self.dsem:
            self.dsem[semname] = [self.es.enter_context(self.nc.semaphore("d_" + semname)), 0]
        s = self.dsem[semname]
        if s[1] > 0:
            self._waits(eng, [("d_" + semname, s[0], s[1])])
        s[1] += 16
        sem = s[0]
        o, i = out.ap, in_.ap
        self.ops[eng].append(lambda e, o=o, i=i, sem=sem: e.dma_start(out=o, in_=i).then_inc(sem, 16))
        self._record(("d_" + semname, sem, s[1]), reads, writes)

    def collective(self, ins, outs, semname):
        reads, writes = ins.k, outs.k
        self._waits("pool", self._deps(reads, writes))
        if semname not in self.dsem:
            self.dsem[semname] = [self.es.enter_context(self.nc.semaphore("d_" + semname)), 0]
        s = self.dsem[semname]
        if s[1] > 0:
            self._waits("pool", [("d_" + semname, s[0], s[1])])
        s[1] += 1
        sem = s[0]
        i, o = ins.ap, outs.ap
        self.ops["pool"].append(lambda e, i=i, o=o, sem=sem: e.collective_compute(
            "AllGather", ALU.bypass, replica_groups=PAIRS, ins=[i], outs=[o]).then_inc(sem))
        self._record(("d_" + semname, sem, s[1]), reads, writes)

    def mm(self, out, lhsT, rhs, start=True, stop=True):
        self.op("pe", lambda e: e.matmul(out.ap, lhsT.ap, rhs.ap, start=start, stop=stop),
                reads=lhsT.k + rhs.k, writes=out.k)

    def act(self, out, in_, func, scale=1.0, bias=0.0, eng="act"):
        r = list(in_.k)
        sc = scale
        bi = bias
        if isinstance(scale, V):
            r += scale.k
            sc = scale.ap
        if isinstance(bias, V):
            r += bias.k
            bi = bias.ap
        self.op("act", lambda e: e.activation(out=out.ap, in_=in_.ap, func=func, bias=bi, scale=sc),
                reads=r, writes=out.k)

    def tt(self, eng, out, a, b, op):
        self.op(eng, lambda e: e.tensor_tensor(out=out.ap, in0=a.ap, in1=b.ap, op=op),
                reads=a.k + b.k, writes=out.k)

    def ts(self, eng, out, a, s1, s2, op0, op1=None):
        r = list(a.k)
        x1, x2 = s1, s2
        if isinstance(s1, V):
            r += s1.k
            x1 = s1.ap
        if isinstance(s2, V):
            r += s2.k
            x2 = s2.ap
        if op1 is None:
            self.op(eng, lambda e: e.tensor_scalar(out.ap, a.ap, x1, None, op0), reads=r, writes=out.k)
        else:
            self.op(eng, lambda e: e.tensor_scalar(out.ap, a.ap, x1, x2, op0, op1), reads=r, writes=out.k)

    def stt(self, out, a, s, b, op0, op1):
        r = a.k + b.k
        x = s
        if isinstance(s, V):
            r = r + s.k
            x = s.ap
        self.op("dve", lambda e: e.scalar_tensor_tensor(out=out.ap, in0=a.ap, scalar=x, in1=b.ap, op0=op0, op1=op1),
                reads=r, writes=out.k)

    def copy(self, eng, out, a):
        if eng == "act":
            self.op("act", lambda e: e.copy(out.ap, a.ap), reads=a.k, writes=out.k)
        else:
            self.op(eng, lambda e: e.tensor_copy(out.ap, a.ap), reads=a.k, writes=out.k)

    def recip(self, out, a):
        self.op("dve", lambda e: e.reciprocal(out.ap, a.ap), reads=a.k, writes=out.k)

    def memset(self, eng, out, val):
        self.op(eng, lambda e: e.memset(out.ap, val), reads=(), writes=out.k)

    def emit(self, final_waits):
        nc = self.nc
        engmap = {"pe": "tensor", "act": "scalar", "dve": "vector", "pool": "gpsimd", "sp": "sync"}
        for (sname, sem, val) in final_waits:
            self.ops["sp"].append(lambda e, sem=sem, val=val: e.wait_ge(sem, val))
        with nc.Block() as block:
            for en in self.ENG:
                ops = self.ops[en]

                def body(e, ops=ops):
                    for f in ops:
                        f(e)
                getattr(block, engmap[en])(body)


class Ring:
    def __init__(self, bufs):
        self.bufs = bufs
        self.i = 0

    def next(self):
        b = self.bufs[self.i % len(self.bufs)]
        self.i += 1
        return b


def build(layer_ids, stage="full"):
    nc = bass.Bass("TRN2", target_bir_lowering=False)
    es = ExitStack()
    P = Prog(nc, es)

    def dram_in(name, shape, dt=F32):
        return nc.dram_tensor(name, list(shape), dt, kind="ExternalInput")

    xin = dram_in("xin", [128, 8 * NTOK])
    cin = dram_in("cin", [128, 16])
    cmat_d = dram_in("cmat", [128, 3 * 128])
    e2_d = dram_in("e2", [2, 128])
    hval_d = dram_in("hval", [128, 2])
    cos_d = dram_in("cosd", [128, NLAT + 512 + 4096])
    sin_d = dram_in("sind", [128, NLAT + 512 + 4096])
    yout = nc.dram_tensor("yout", [128, 8 * NTOK], F32, kind="ExternalOutput")
    L = {}
    for i in layer_ids:
        kind = i % 3
        d = {}
        d["wmod"] = dram_in(f"wmod{i}", [128, 8 * 6144])
        d["bmod"] = dram_in(f"bmod{i}", [128, 48])
        d["gat"] = dram_in(f"gat{i}", [128, 8])
        d["gff"] = dram_in(f"gff{i}", [128, 8])
        kvw = 1024 if kind == 0 else 256
        d["wq"] = dram_in(f"wq{i}", [128, 8 * 1024])
        d["wk"] = dram_in(f"wk{i}", [128, 8 * kvw])
        d["wv"] = dram_in(f"wv{i}", [128, 8 * kvw])
        d["wo"] = dram_in(f"wo{i}", [128, 8 * 1024])
        d["gqk"] = dram_in(f"gqk{i}", [128, 2])
        if kind == 0:
            d["btab"] = dram_in(f"btab{i}", [16 * 128, 7 * 128])
            d["bx"] = dram_in(f"bx{i}", [16 * 128, 2 * 128])
            d["mrow"] = dram_in(f"mrow{i}", [2, 16 * 8 * 2])
        if kind == 1:
            d["swab"] = dram_in(f"swab{i}", [128, 4 * 512])
            d["sink"] = dram_in(f"sink{i}", [1, 16])
        d["wup"] = dram_in(f"wup{i}", [NF * 128, 2 * 8 * 128])
        d["cw"] = dram_in(f"cw{i}", [128, 44 * 4])
        d["wdn"] = dram_in(f"wdn{i}", [8 * 128, NF * 128])
        L[i] = d

    hown = nc.dram_tensor("hown", [4 * 128, 8 * 512], BF16)
    hctx = nc.dram_tensor("hctx", [128, 8 * NCTX], BF16)
    snd = nc.dram_tensor("snd", [2 * 128, 8 * 256], BF16)
    rcv = nc.dram_tensor("rcv", [4 * 128, 8 * 256], BF16)
    rcvg = nc.dram_tensor("rcvg", [4 * 2 * 128, 8 * 512], BF16)
    snd2 = nc.dram_tensor("snd2", [2, 1024], BF16)
    rcv2 = nc.dram_tensor("rcv2", [4, 1024], BF16)

    def dview(t, key):
        return V(t, [key])

    xT_t = es.enter_context(nc.sbuf_tensor("xT", [128, 8 * NTOK], F32))
    ar_t = es.enter_context(nc.sbuf_tensor("arena", [128, ARENA_F32], F32))
    cs_t = es.enter_context(nc.sbuf_tensor("consts", [128, 768], F32))
    ps_t = es.enter_context(nc.psum_tensor("ps", [128, 8 * 512], F32))
    xT = Buf(xT_t, 0, [8, NTOK], F32, "x")
    PS = Buf(ps_t, 0, [8, 512], F32, "ps")
    cmat = Buf(cs_t, 0, [3, 128], F32, "cs", cell=64)
    csil = Buf(cs_t, 384, [8, 2], F32, "cs", cell=64)
    hval = Buf(cs_t, 400, [2], F32, "cs", cell=64)
    e2b = Buf(cs_t, 402, [128], BF16, "cs", cell=64)
    gqk = Buf(cs_t, 466, [2], F32, "cs", cell=64)
    sinkt = Buf(cs_t, 468, [16], F32, "cs", cell=64)
    modb = Buf(cs_t, 484, [48, 2], F32, "cs", cell=64)
    gs = Buf(cs_t, 580, [2, 8, 2], F32, "cs", cell=64)
    gtmp = Buf(cs_t, 612, [2, 8], F32, "cs", cell=64)
    bmodt = Buf(cs_t, 628, [48], F32, "cs", cell=64)
    halo = Buf(cs_t, 676, [8, 2], BF16, "cs", cell=64)
    halor = Buf(cs_t, 684, [2, 8], BF16, "cs", cell=64)
    bnd = Buf(cs_t, 692, [2, 8], BF16, "cs", cell=64)
    h2sv = Buf(cs_t, 704, [8, 1], BF16, "cs", cell=64)
    csilb = Buf(cs_t, 712, [8, 2], BF16, "cs", cell=64)
    ones_v = cmat[:, 0, :]
    bdiag_v = cmat[:, 1, :]
    rot_v = cmat[:, 2, :]

    class Arena:
        def __init__(self):
            self.off = 0

        def reset(self, off=0):
            self.off = (off + 255) // 256 * 256

        def alloc(self, shape, dtype):
            n32 = (_prod(shape) * (2 if dtype == BF16 else 4) + 3) // 4
            n32 = (n32 + 255) // 256 * 256
            b = Buf(ar_t, self.off, shape, dtype, "ar", cell=1024)
            self.off += n32
            assert self.off <= ARENA_F32, f"arena overflow {self.off}"
            return b

    AR = Arena()

    for c_ in range(8):
        P.dma(xT[:, c_, :], V(xin[:, c_ * NTOK:(c_ + 1) * NTOK], []), f"ldx{c_ % 4}", eng=("sp" if c_ % 2 == 0 else "act"))
    P.dma(cmat.all(), V(cmat_d[:, :].rearrange("p (a b) -> p a b", a=3), []), "ldc")
    P.dma(csil.all(), V(cin[:, :].rearrange("p (a b) -> p a b", a=8), []), "ldc")
    P.dma(hval.all(), V(hval_d[:, :], []), "ldc")
    P.memset("dve", e2b.all(), 0.0)
    P.dma(e2b[0:2, :], V(e2_d[:, :], []), "ldp", eng="pool")
    P.act(csil.all(), csil.all(), AF.Silu)
    P.copy("dve", csilb.all(), csil.all())

    def norm_cols(tok0, n, which, col, out_fn, scr):
        sqr, tbr, lnt, rstd, stat = scr
        stat_v = stat[:, 0:n]
        for c in range(8):
            xs = xT[:, c, tok0:tok0 + n]
            sq = sqr.next()[:, 0:n]
            P.tt("dve", sq, xs, xs, ALU.mult)
            P.mm(stat_v, ones_v, sq, start=(c == 0), stop=(c == 7))
        ln_v = lnt[:, 0:n]
        rs_v = rstd[:, 0:n]
        P.act(ln_v, stat_v, AF.Ln, scale=1.0 / D, bias=EPS_V)
        P.act(rs_v, ln_v, AF.Exp, scale=-0.5)
        sh = 0 if which == 0 else 3
        for c in range(8):
            xs = xT[:, c, tok0:tok0 + n]
            t = tbr.next()[:, 0:n]
            P.tt("dve", t, xs, rs_v, ALU.mult)
            P.act(out_fn(c), t, AF.Identity, scale=gs[:, which, c, col:col + 1], bias=modb[:, sh * 8 + c, col:col + 1])

    epsb = Buf(cs_t, 700, [1], F32, "cs", cell=64)
    P.memset("dve", epsb.all(), EPS)
    EPS_V = epsb[:, 0:1]

    def emit_mod(i):
        d = L[i]
        AR.reset()
        wmr = Ring([AR.alloc([8, 512], F32) for _ in range(4)])
        wmbr = Ring([AR.alloc([8, 512], BF16) for _ in range(2)])
        P.dma(bmodt.all(), V(d["bmod"][:, :], []), "ldc")
        P.dma(gtmp[:, 0, :], V(d["gat"][:, :], []), "ldc")
        P.dma(gtmp[:, 1, :], V(d["gff"][:, :], []), "ldc")
        wsrc = d["wmod"][:, :].rearrange("p (k n) -> p k n", k=8)
        for pc in range(12):
            wm = wmr.next()
            P.dma(wm.all(), V(wsrc[:, :, pc * 512:(pc + 1) * 512], []), f"wm{pc % 4}", eng=("sp" if pc % 2 == 0 else "act"))
            wmb = wmbr.next()
            P.copy("act" if pc % 2 == 0 else "dve", wmb.all(), wm.all())
            for jj in range(4):
                j = pc * 4 + jj
                o = PS[:, 0, 2 * j:2 * j + 2]
                for k in range(8):
                    P.mm(o, wmb[:, k, jj * 128:(jj + 1) * 128], csilb[:, k, :], start=(k == 0), stop=(k == 7))
        pv = PS[:, 0, 0:96]
        P.tt("dve", V(modb.ap, modb.all().k),
             V(pv.ap.rearrange("p (j t) -> p j t", t=2), pv.k),
             V(bmodt.ap.unsqueeze(2).broadcast_to([128, 48, 2]), bmodt.all().k), ALU.add)
        for w_ in range(2):
            scv = modb[:, (1 + 3 * w_) * 8:(2 + 3 * w_) * 8, :]
            P.ts("dve", gs[:, w_, :, :], scv, 1.0, None, ALU.add)
            P.tt("dve", gs[:, w_, :, :], gs[:, w_, :, :],
                 V(gtmp.ap[:, w_, :].unsqueeze(2).broadcast_to([128, 8, 2]), gtmp.all().k), ALU.mult)

    def emit_attention(i):
        d = L[i]
        kind = i % 3
        last = (i == 3)
        is_na = (kind == 0)
        rope = (kind != 0)
        npass = 4 if is_na else 2
        NKC = 34 if kind == 2 else 22
        nQ = 2 if is_na else 4
        NH = 4 if is_na else 2
        VW = NH * 64
        P.dma(gqk.all(), V(d["gqk"][:, :], []), "ldc")
        if kind == 1:
            P.dma(sinkt[64:65, :], V(d["sink"][:, :], []), "ldc")
            P.act(sinkt[64:65, :], sinkt[64:65, :], AF.Exp)

        AR.reset()
        sqr = Ring([AR.alloc([512], F32) for _ in range(3)])
        tbr = Ring([AR.alloc([512], F32) for _ in range(3)])
        lnt = AR.alloc([512], F32)
        rstd = AR.alloc([512], F32)
        hst = Ring([AR.alloc([8, 512], BF16) for _ in range(2)])
        scr = (sqr, tbr, lnt, rstd, PS[:, 0, :])
        scr = (sqr, tbr, lnt, rstd, Buf(ps_t, 0, [512], F32, "ps"))
        def hown_blk(tb):
            return hown[tb * 128:(tb + 1) * 128, :].rearrange("p (c t) -> p c t", c=8)

        def rcvg_blk(tb, r):
            return rcvg[(tb * 2 + r) * 128:(tb * 2 + r + 1) * 128, :].rearrange("p (c t) -> p c t", c=8)
        hctx_v = hctx[:, :].rearrange("p (c t) -> p c t", c=8)
        snd_v = snd[:, :].rearrange("(s p) (c t) -> s p c t", s=2, c=8)
        for tb in range(5):
            n = 512 if tb < 4 else 256
            hs = hst.next()
            norm_cols(tb * 512, n, 0, 0 if tb < 4 else 1, lambda c, hs=hs, n=n: hs[:, c, 0:n], scr)
            if tb < 4:
                P.dma(V(hown_blk(tb), [("hown", tb)]), hs.all(), f"hst{tb % 2}")
                if kind == 2:
                    P.collective(V(hown[tb * 128:(tb + 1) * 128, :], [("hown", tb)]),
                                 V(rcvg[tb * 256:(tb + 1) * 256, :], [("rcvg", tb)]), "cc")
                if kind != 2 and tb == 0:
                    P.dma(V(snd_v[0], [("snd", 0)]), hs[:, :, 0:256], f"hst{tb % 2}")
                if kind != 2 and tb == 3:
                    P.dma(V(snd_v[1], [("snd", 1)]), hs[:, :, 256:512], f"hst{tb % 2}")
            else:
                P.dma(V(hctx_v, [("hctx", 0)]), hs[:, :, 0:256], f"hst{tb % 2}")
        if kind != 2:
            P.collective(V(snd[:, :], [("snd", 0), ("snd", 1)]), V(rcv[:, :], [("rcv", 0)]), "cc")
        rcv_v = rcv[:, :].rearrange("(r s p) (c t) -> r s p c t", r=2, s=2, c=8)

        blocks = []
        if kind != 2:
            blocks.append((V(rcv_v[0, 1], [("rcv", 0)]), 256, 0, None, NLAT if rope else None))
            for tb in range(4):
                blocks.append((V(hown_blk(tb), [("hown", tb)]), 512, 2 + 4 * tb, tb * 512,
                               tb * 512 if rope else None))
            blocks.append((V(rcv_v[1, 0], [("rcv", 0)]), 256, 18, None, NLAT + 256 if rope else None))
            blocks.append((V(hctx_v, [("hctx", 0)]), 256, 20, None if last else 2048, None))
        else:
            for r in range(2):
                for tb in range(4):
                    kc0_ = 16 * r + 4 * tb
                    blocks.append((V(rcvg_blk(tb, r), [("rcvg", tb)]), 512, kc0_, None,
                                   GA_COS_OFF + kc0_ * 128))
            for tb in range(4):
                blocks.append((V(hown_blk(tb), [("hown", tb)]), 512, None, tb * 512, tb * 512))
            blocks.append((V(hctx_v, [("hctx", 0)]), 256, 32, 2048, None))

        wq_v = d["wq"][:, :].rearrange("p (k n) -> p k n", k=8)
        wk_v = d["wk"][:, :].rearrange("p (k n) -> p k n", k=8)
        wv_v = d["wv"][:, :].rearrange("p (k n) -> p k n", k=8)
        wo_v = d["wo"][:, :].rearrange("p (k n) -> p k n", k=8)

        for pz in range(npass):
            AR.reset()
            KT = AR.alloc([NH, NKC * 128], BF16)
            VT = AR.alloc([NKC, NH, 65], BF16)
            QT = AR.alloc([nQ, NTOK], BF16)
            persist_off = AR.off
            P.memset("dve", VT[:, :, :, 64:65], 1.0)
            for h_ in range(NH):
                for c0_ in range(0, NKC * 128, 2048):
                    P.memset("dve", KT[:, h_, c0_:min(c0_ + 2048, NKC * 128)], 0.0)
            wqr = Ring([AR.alloc([8, 128], BF16) for _ in range(2)])
            wkr = Ring([AR.alloc([8, 128], BF16) for _ in range(2)])
            wvb = AR.alloc([8, 256], BF16)
            hbr = Ring([AR.alloc([8, 512], BF16) for _ in range(2)])
            qcr = Ring([AR.alloc([512], F32) for _ in range(2)])
            nr_ = 2
            sq2 = Ring([AR.alloc([512], F32) for _ in range(2)])
            ln2 = AR.alloc([512], F32)
            rs2 = Ring([AR.alloc([512], F32) for _ in range(nr_)])
            qnr = Ring([AR.alloc([512], F32) for _ in range(2)])
            t1r = Ring([AR.alloc([512], F32) for _ in range(nr_)])
            t2r = Ring([AR.alloc([512], F32) for _ in range(nr_)])
            csr = Ring([AR.alloc([2, 512], F32) for _ in range(4)])
            P.dma(wvb[:, :, 0:VW], V(wv_v[:, :, pz * VW:(pz + 1) * VW], []), "wvb", eng="pool")
            projps = Ring([Buf(ps_t, 512 * b, [512], F32, "ps") for b in (1, 2)])
            st2ps = Buf(ps_t, 512 * 3, [512], F32, "ps")
            ropeps = Buf(ps_t, 512 * 4, [512], F32, "ps")
            vps = Ring([Buf(ps_t, 512 * b, [512], F32, "ps") for b in (5, 6)])

            pendB = deque()
            pendC = deque()

            def post_A(ps, n, gcol, dst, cs):
                qc = qcr.next()[:, 0:n]
                P.copy("act", qc, ps[:, 0:n])
                sq = sq2.next()[:, 0:n]
                P.tt("dve", sq, qc, qc, ALU.mult)
                return (n, gcol, dst, cs, qc, sq)

            def post_B(task):
                n, gcol, dst, cs, qc, sq = task
                P.mm(st2ps[:, 0:n], bdiag_v, sq)
                lv = ln2[:, 0:n]
                P.act(lv, st2ps[:, 0:n], AF.Ln, scale=1.0 / 64, bias=EPS_V)
                rs = rs2.next()[:, 0:n]
                P.act(rs, lv, AF.Exp, scale=-0.5)
                if cs is None:
                    for (psl, dv) in dst:
                        P.stt(dv, V(qc.ap[psl], qc.k), gqk[psl, gcol:gcol + 1], V(rs.ap[psl], rs.k), ALU.mult, ALU.mult)
                    return None
                qn = qnr.next()[:, 0:n]
                P.stt(qn, qc, gqk[:, gcol:gcol + 1], rs, ALU.mult, ALU.mult)
                return (n, dst, cs, qn)

            def post_C(task):
                n, dst, cs, qn = task
                P.mm(ropeps[:, 0:n], rot_v, qn)
                t1 = t1r.next()[:, 0:n]
                P.tt("dve", t1, qn, cs[:, 0, 0:n], ALU.mult)
                t2 = t2r.next()[:, 0:n]
                P.tt("dve", t2, ropeps[:, 0:n], cs[:, 1, 0:n], ALU.mult)
                for (psl, dv) in dst:
                    P.tt("dve", dv, V(t1.ap[psl], t1.k), V(t2.ap[psl], t2.k), ALU.add)

            def post_tick(newtask):
                if pendC:
                    post_C(pendC.popleft())
                if pendB:
                    r_ = post_B(pendB.popleft())
                    if r_ is not None:
                        pendC.append(r_)
                if newtask is not None:
                    pendB.append(newtask)

            def qk_post(ps, n, gcol, dst, cs):
                post_tick(post_A(ps, n, gcol, dst, cs))

            if is_na:
                qcols = [pz * 256 + c * 128 for c in range(2)]
                kcols = [pz * 256 + c * 128 for c in range(2)]
            else:
                qcols = [(4 * pz + c) * 128 for c in range(4)]
                kcols = [pz * 128]

            for (src, n, kc0, qt0, cscol) in blocks:
                hb = hbr.next()
                P.dma(hb[:, :, 0:n], src, f"hb{(hbr.i - 1) % 2}")
                cs = None
                if cscol is not None:
                    cs = csr.next()
                    P.dma(cs[:, 0, 0:n], V(cos_d[:, cscol:cscol + n], []), f"cs{(csr.i - 1) % 4}")
                    P.dma(cs[:, 1, 0:n], V(sin_d[:, cscol:cscol + n], []), f"cs{(csr.i - 1) % 4}")
                if qt0 is not None:
                    for ci, qc_ in enumerate(qcols):
                        wq = wqr.next()
                        P.dma(wq.all(), V(wq_v[:, :, qc_:qc_ + 128], []), f"wq{(wqr.i - 1) % 2}", eng="pool")
                        pp = projps.next()
                        for k in range(8):
                            P.mm(pp[:, 0:n], wq[:, k, :], hb[:, k, 0:n], start=(k == 0), stop=(k == 7))
                        qk_post(pp, n, 0, [(slice(0, 128), QT[:, ci, qt0:qt0 + n])], cs)
                if kc0 is not None:
                    for ci, kc_ in enumerate(kcols):
                        wk = wkr.next()
                        P.dma(wk.all(), V(wk_v[:, :, kc_:kc_ + 128], []), f"wk{(wkr.i - 1) % 2}", eng="pool")
                        pp = projps.next()
                        for k in range(8):
                            P.mm(pp[:, 0:n], wk[:, k, :], hb[:, k, 0:n], start=(k == 0), stop=(k == 7))
                        qk_post(pp, n, 1, [(slice(0, 64), KT[0:64, 2 * ci, kc0 * 128:kc0 * 128 + n]),
                                           (slice(64, 128), KT[64:128, 2 * ci + 1, kc0 * 128:kc0 * 128 + n])], cs)
                    for ts_ in range(n // 128):
                        vp = vps.next()
                        for k in range(8):
                            P.mm(vp[:, 0:VW], hb[:, k, ts_ * 128:(ts_ + 1) * 128], wvb[:, k, 0:VW], start=(k == 0), stop=(k == 7))
                        P.copy("act", VT[:, kc0 + ts_, :, 0:64],
                               V(vp.ap[:, 0:VW].rearrange("p (h d) -> p h d", h=NH), vp[:, 0:VW].k))

            while pendB or pendC:
                post_tick(None)

            AR.reset(persist_off)
            OT = AR.alloc([8, 512], BF16)
            wo_rows = 2 if is_na else 4
            WO = AR.alloc([wo_rows, 1024], BF16)
            P.dma(WO.all(), V(wo_v[:, pz * wo_rows:(pz + 1) * wo_rows, :], []), "wo", eng="pool")
            tmpr = Ring([AR.alloc([1024], F32) for _ in range(2)])
            if is_na:
                ptr = Ring([AR.alloc([1024], BF16) for _ in range(2)])
            else:
                ptr4 = Ring([AR.alloc([512], BF16) for _ in range(4)])
            rden = AR.alloc([512], F32)
            ounr = Ring([AR.alloc([512], F32) for _ in range(2)])
            if is_na:
                BT = AR.alloc([4, 896], F32)
                BX = AR.alloc([4, 2, 128], F32)
                MR = AR.alloc([16, 8, 2], BF16)
                bt_v = d["btab"][:, :].rearrange("(h p) n -> p h n", p=128)
                bx_v = d["bx"][:, :].rearrange("(h p) (s n) -> p h s n", p=128, s=2)
                P.dma(BT.all(), V(bt_v[:, pz * 4:pz * 4 + 4, :], []), "bt")
                P.dma(BX.all(), V(bx_v[:, pz * 4:pz * 4 + 4], []), "bt")
                P.memset("dve", MR.all(), 0.0)
                P.dma(MR[0:2], V(d["mrow"][:, :].rearrange("p (a b c) -> p a b c", a=16, b=8), []), "mr", eng="pool")
            if kind == 1:
                SB = AR.alloc([4, 512], F32)
                P.dma(SB.all(), V(d["swab"][:, :].rearrange("p (a b) -> p a b", a=4), []), "bt")
            sps = Ring([Buf(ps_t, 512 * b, [1024], F32, "ps") for b in (0, 2)])
            ops_ = Ring([Buf(ps_t, 512 * b, [512], F32, "ps") for b in (4, 5)])
            bcps = Buf(ps_t, 512 * 6, [512], F32, "ps")
            yps = Ring([Buf(ps_t, 512 * b, [512], F32, "ps") for b in (7, 6)])

            def normalize(po, ncols, heads_dst, sinkrow=None):
                dn = po[64:65, 0:ncols]
                rd = rden[64:65, 0:ncols]
                P.act(rd, dn, AF.Ln)
                P.act(rd, rd, AF.Exp, scale=-1.0)
                P.mm(bcps[0:64, 0:ncols], cmat[64:65, 0, 0:64], rd)
                ou = ounr.next()[0:64, 0:ncols]
                P.copy("act", ou, po[0:64, 0:ncols])
                for (c0, chunk, half, t0) in heads_dst:
                    P.tt("dve", OT[half * 64:half * 64 + 64, chunk, t0:t0 + 128],
                         V(ou.ap[:, c0:c0 + 128], ou.k), V(bcps.ap[0:64, c0:c0 + 128], bcps[0:64, 0:ncols].k), ALU.mult)

            def out_proj(tok0, ntok, col):
                for dc in range(8):
                    yp = yps.next()
                    nk = wo_rows
                    for c in range(nk):
                        P.mm(yp[:, 0:ntok], WO[:, c, dc * 128:(dc + 1) * 128], OT[:, (c if is_na else c), 0:ntok],
                             start=(c == 0), stop=(c == nk - 1))
                    xs = xT[:, dc, tok0:tok0 + ntok]
                    P.stt(xs, yp[:, 0:ntok], modb[:, 16 + dc, col:col + 1], xs, ALU.mult, ALU.add)

            qblocks = list(range(16)) + ([] if last else [16, 17])
            pend = deque()

            pend_fin = deque()

            def finish_group(qb, po, heads, use_sink, g, lastg):
                rd = rden[64:65, 0:512]
                if use_sink:
                    sinkrow = V(sinkt.ap[64:65, 4 * g:4 * g + 4].unsqueeze(2).broadcast_to([1, 4, 128]), sinkt[64:65, :].k)
                    dn = V(po.ap[64:65, 0:512].rearrange("p (a b) -> p a b", a=4), po[64:65, 0:512].k)
                    rd3 = V(rden.ap[64:65, 0:512].rearrange("p (a b) -> p a b", a=4), rden[64:65, 0:512].k)
                    P.tt("dve", rd3, dn, sinkrow, ALU.add)
                    P.act(rd, rd, AF.Ln)
                else:
                    P.act(rd, po[64:65, 0:512], AF.Ln)
                P.act(rd, rd, AF.Exp, scale=-1.0)
                ou = ounr.next()[0:64, 0:512]
                P.copy("act", ou, po[0:64, 0:512])

                def part_b():
                    P.mm(bcps[0:64, 0:512], cmat[64:65, 0, 0:64], rd)
                    for (c0, chunk, half, t0) in heads:
                        P.tt("dve", OT[half * 64:half * 64 + 64, chunk, t0:t0 + 128],
                             V(ou.ap[:, c0:c0 + 128], ou.k), V(bcps.ap[0:64, c0:c0 + 128], bcps[0:64, 0:512].k), ALU.mult)
                    if lastg and qb % 4 == 3 and qb < 16:
                        out_proj((qb // 4) * 512, 512, 0)
                    if lastg and qb == 17:
                        out_proj(2048, 256, 1)
                pend_fin.append([0, part_b])

            def fin_tick(flush=False):
                while pend_fin and (flush or pend_fin[0][0] >= 1):
                    pend_fin.popleft()[1]()
                for e_ in pend_fin:
                    e_[0] += 1

            if is_na:
                SK = 1
                state = {"po": None}

                def na_stage1(qb, hl):
                    is_ctxq = qb >= 16
                    lm = qb
                    tq0 = qb * 128
                    chunk, half = hl // 2, hl % 2
                    hs = slice(half * 64, half * 64 + 64)
                    if is_ctxq:
                        slots = [(20, None), (21, None)]
                    else:
                        slots = [(lm + s_, s_) for s_ in range(5)] + [(20, 5), (21, 6)]
                        if lm == 0:
                            slots.append((5, 7))
                        if lm == 15:
                            slots.append((14, 7))
                    nsl = len(slots)
                    sp_ = sps.next()
                    if not is_ctxq:
                        for b0 in range(0, nsl, 4):
                            b1 = min(b0 + 4, nsl)
                            mr = V(MR.ap[:, lm, b0:b1, :].unsqueeze(3).broadcast_to([128, b1 - b0, 2, 64]), MR.all().k)
                            P.mm(sp_[:, b0 * 128:b1 * 128], e2b.all(), mr, start=True, stop=False)
                    for si, (kc, _) in enumerate(slots):
                        P.mm(sp_[:, si * 128:(si + 1) * 128], KT[:, hl, kc * 128:(kc + 1) * 128],
                             QT[:, chunk, tq0:tq0 + 128], start=is_ctxq,
                             stop=(True if is_ctxq else (si == nsl - 1 or si % 4 == 3)))
                    pt = ptr.next()
                    if is_ctxq:
                        P.act(pt[:, 0:nsl * 128], sp_[:, 0:nsl * 128], AF.Exp, scale=0.125)
                    else:
                        tm = tmpr.next()
                        P.stt(tm[:, 0:896], sp_[:, 0:896], 0.125, BT[:, hl, :], ALU.mult, ALU.add)
                        if nsl == 8:
                            P.stt(tm[:, 896:1024], sp_[:, 896:1024], 0.125, BX[:, hl, 0 if lm == 0 else 1, :], ALU.mult, ALU.add)
                        P.act(pt[:, 0:nsl * 128], tm[:, 0:nsl * 128], AF.Exp)
                    return (qb, hl, slots, pt)

                def na_stage2(item):
                    qb, hl, slots, pt = item
                    nsl = len(slots)
                    if hl == 0:
                        state["po"] = ops_.next()
                    po = state["po"]
                    for si, (kc, _) in enumerate(slots):
                        P.mm(po[0:65, hl * 128:(hl + 1) * 128], VT[:, kc, hl, :], pt[:, si * 128:(si + 1) * 128],
                             start=(si == 0), stop=(si == nsl - 1))
                    if hl == 3:
                        otc = (qb % 4) * 128
                        finish_group(qb, po, [(h_ * 128, h_ // 2, h_ % 2, otc) for h_ in range(4)], False, 0, True)

                for qb in qblocks:
                    for hl in range(4):
                        pend.append(na_stage1(qb, hl))
                        fin_tick()
                        if len(pend) > SK:
                            na_stage2(pend.popleft())
                while pend:
                    fin_tick(flush=True)
                    na_stage2(pend.popleft())
                fin_tick(flush=True)
            else:
                SK = 2
                state = {"po": None}
                sps1 = Ring([Buf(ps_t, 512 * b, [512], F32, "ps") for b in (0, 1, 2, 3)])

                def g_stage1(qb, g, si, kc, bvar, nsl):
                    tq0 = qb * 128
                    sp_ = sps1.next()
                    qsl = QT[:, 0:4, tq0:tq0 + 128]
                    P.mm(sp_[:, 0:512], KT[:, g, kc * 128:(kc + 1) * 128], qsl)
                    pt = ptr4.next()
                    if bvar is None:
                        P.act(pt[:, 0:512], sp_[:, 0:512], AF.Exp, scale=0.125)
                    else:
                        tm = tmpr.next()
                        P.stt(tm[:, 0:512], sp_[:, 0:512], 0.125, SB[:, bvar, :], ALU.mult, ALU.add)
                        P.act(pt[:, 0:512], tm[:, 0:512], AF.Exp)
                    return (qb, g, si, kc, nsl, pt)

                def g_stage2(item):
                    qb, g, si, kc, nsl, pt = item
                    if si == 0:
                        state["po"] = ops_.next()
                    po = state["po"]
                    P.mm(po[0:65, 0:512], VT[:, kc, g, :], pt[:, 0:512], start=(si == 0), stop=(si == nsl - 1))
                    if si == nsl - 1:
                        otc = (qb % 4) * 128
                        heads = [(i_ * 128, 2 * g + i_ // 2, i_ % 2, otc) for i_ in range(4)]
                        finish_group(qb, po, heads, kind == 1, 2 * pz + g, g == 1)

                for qb in qblocks:
                    is_ctxq = qb >= 16
                    for g in range(2):
                        if is_ctxq:
                            slots = [(NKC - 2, None), (NKC - 1, None)]
                        elif kind == 1:
                            slots = [(qb + 1, 2 if qb == 0 else 0), (qb + 2, None), (qb + 3, 3 if qb == 15 else 1), (20, None), (21, None)]
                        else:
                            slots = [(kc, None) for kc in range(NKC)]
                        for si, (kc, bvar) in enumerate(slots):
                            pend.append(g_stage1(qb, g, si, kc, bvar, len(slots)))
                            fin_tick()
                            if len(pend) > SK:
                                g_stage2(pend.popleft())
                while pend:
                    fin_tick(flush=True)
                    g_stage2(pend.popleft())
                fin_tick(flush=True)

    def emit_ffn(i):
        d = L[i]
        last = (i == 3)
        AR.reset()
        h2 = AR.alloc([8, 1026], BF16)
        actT = AR.alloc([NF, 1024], BF16)
        wur = Ring([AR.alloc([2, 8, 128], BF16) for _ in range(2)])
        wdr = Ring([AR.alloc([NF, 128], BF16) for _ in range(2)])
        uar = Ring([AR.alloc([1026], F32) for _ in range(2)])
        ugr = Ring([AR.alloc([1026], F32) for _ in range(2)])
        tar = Ring([AR.alloc([1024], F32) for _ in range(2)])
        tgr = Ring([AR.alloc([1024], F32) for _ in range(2)])
        sqr = Ring([AR.alloc([512], F32) for _ in range(2)])
        tbr = Ring([AR.alloc([512], F32) for _ in range(2)])
        lnt = AR.alloc([512], F32)
        rstd = AR.alloc([512], F32)
        CW = AR.alloc([44, 4], F32)
        scr = (sqr, tbr, lnt, rstd, Buf(ps_t, 0, [512], F32, "ps"))
        P.dma(CW.all(), V(d["cw"][:, :].rearrange("p (f t) -> p f t", f=44), []), "ldc")
        wup_v = d["wup"][:, :].rearrange("(f p) (t k n) -> f p t k n", p=128, t=2, k=8)
        wdn_v = d["wdn"][:, :].rearrange("(c p) (f n) -> c p f n", p=128, f=NF)

        norm_cols(0, 1, 1, 0, lambda c: bnd[:, 0, c:c + 1], scr)
        norm_cols(NLAT - 1, 1, 1, 0, lambda c: bnd[:, 1, c:c + 1], scr)
        snd2_v = snd2[:, :].rearrange("s (p c) -> s p c", p=128)
        rcv2_v = rcv2[:, :].rearrange("r (p c) -> r p c", p=128)
        P.dma(V(snd2_v[0], [("snd2", 0)]), bnd[:, 0, :], "bnd")
        P.dma(V(snd2_v[1], [("snd2", 0)]), bnd[:, 1, :], "bnd")
        P.collective(V(snd2[:, :], [("snd2", 0)]), V(rcv2[:, :], [("rcv2", 0)]), "cc")
        P.dma(halor[:, 0, :], V(rcv2_v[1], [("rcv2", 0)]), "hal")
        P.dma(halor[:, 1, :], V(rcv2_v[2], [("rcv2", 0)]), "hal")
        for s_ in range(2):
            P.ts("dve", halo[:, :, s_], halor[:, s_, :], hval[:, s_:s_ + 1], None, ALU.mult)

        aps = Ring([Buf(ps_t, 512 * b, [512], F32, "ps") for b in (1, 2)])
        gps = Ring([Buf(ps_t, 512 * b, [512], F32, "ps") for b in (3, 4)])
        yps = Ring([Buf(ps_t, 512 * b, [512], F32, "ps") for b in (5, 6)])
        eps2 = Buf(ps_t, 512 * 7, [512], F32, "ps")

        segs = ([] if last else [(2048, 256, 1, None, None)]) + [(0, 1024, 0, 0, None), (1024, 1024, 0, None, 1)]
        for (s0, n, col, lh, rh) in segs:
            lo = s0
            hi = s0 + n + 1 if (rh is None and col == 0) else s0 + n
            t = lo
            while t < hi:
                m = min(512, hi - t)
                o0 = t - (s0 - 1)
                norm_cols(t, m, 1, col, lambda c, o0=o0, m=m: h2[:, c, o0:o0 + m], scr)
                t += m
            if col == 1:
                P.memset("dve", h2[:, :, 0:1], 0.0)
                P.memset("dve", h2[:, :, n + 1:n + 2], 0.0)
            if lh is not None:
                P.copy("dve", h2[:, :, 0], halo[:, :, 0])
                P.copy("dve", h2sv[:, :, 0], h2[:, :, 1024])
            elif col == 0:
                P.copy("dve", h2[:, :, 0], h2sv[:, :, 0])
            if rh is not None:
                P.copy("dve", h2[:, :, n + 1], halo[:, :, 1])
            pieces = [(0, 512), (512, 1024), (1024, 1026)] if n == 1024 else [(0, 258)]
            pend_f = []
            for f in range(NF):
                wu = wur.next()
                P.dma(wu.all(), V(wup_v[f], []), f"wu{(wur.i - 1) % 2}", eng="pool")
                ua = uar.next()
                ug = ugr.next()
                for (p0, p1) in pieces:
                    m = p1 - p0
                    if m > 2:
                        pa, pg = aps.next(), gps.next()
                        pav, pgv = pa[:, 0:m], pg[:, 0:m]
                    else:
                        pav, pgv = eps2[:, 0:m], eps2[:, 8:8 + m]
                    for k in range(8):
                        P.mm(pav, wu[:, 0, k, :], h2[:, k, p0:p1], start=(k == 0), stop=(k == 7))
                    for k in range(8):
                        P.mm(pgv, wu[:, 1, k, :], h2[:, k, p0:p1], start=(k == 0), stop=(k == 7))
                    P.copy("act", ua[:, p0:p1], pav)
                    P.copy("act", ug[:, p0:p1], pgv)
                ta = tar.next()[:, 0:n]
                P.ts("dve", ta, ua[:, 0:n], CW[:, f, 0:1], CW[:, f, 3:4], ALU.mult, ALU.add)
                P.stt(ta, ua[:, 1:n + 1], CW[:, f, 1:2], ta, ALU.mult, ALU.add)
                P.stt(ta, ua[:, 2:n + 2], CW[:, f, 2:3], ta, ALU.mult, ALU.add)
                tg = tgr.next()[:, 0:n]
                fg = NF + f
                P.ts("dve", tg, ug[:, 0:n], CW[:, fg, 0:1], CW[:, fg, 3:4], ALU.mult, ALU.add)
                P.stt(tg, ug[:, 1:n + 1], CW[:, fg, 1:2], tg, ALU.mult, ALU.add)
                P.stt(tg, ug[:, 2:n + 2], CW[:, fg, 2:3], tg, ALU.mult, ALU.add)
                if pend_f:
                    ta_, tg_, f_ = pend_f.pop()
                    P.act(tg_, tg_, AF.Silu)
                    P.tt("dve", actT[:, f_, 0:n], ta_, tg_, ALU.mult)
                pend_f.append((ta, tg, f))
            while pend_f:
                ta_, tg_, f_ = pend_f.pop()
                P.act(tg_, tg_, AF.Silu)
                P.tt("dve", actT[:, f_, 0:n], ta_, tg_, ALU.mult)
            for dc in range(8):
                wd = wdr.next()
                P.dma(wd.all(), V(wdn_v[dc], []), f"wd{(wdr.i - 1) % 2}", eng="pool")
                for p0 in range(0, n, 512):
                    m = min(512, n - p0)
                    yp = yps.next()
                    for f in range(NF):
                        P.mm(yp[:, 0:m], wd[:, f, :], actT[:, f, p0:p0 + m], start=(f == 0), stop=(f == NF - 1))
                    xs = xT[:, dc, s0 + p0:s0 + p0 + m]
                    P.stt(xs, yp[:, 0:m], modb[:, 40 + dc, col:col + 1], xs, ALU.mult, ALU.add)

    for i in layer_ids:
        emit_mod(i)
        emit_attention(i)
        if stage == "full":
            emit_ffn(i)
    P.dma(V(yout[:, :].rearrange("p (c t) -> p c t", c=8), [("yout", 0)]), xT.all(), "sty")
    fin = P.lastw[("yout", 0)]
    P.emit([fin])
    return nc, es, P


GA_COS_OFF = NLAT + 512


def _chunk_rows(w):
    K, N = w.shape
    return np.ascontiguousarray(w.reshape(K // 128, 128, N).transpose(1, 0, 2).reshape(128, -1))


def _rope_tables():
    t = np.arange(4096)
    row = (t // 64).astype(np.float32)
    col = (t % 64).astype(np.float32)
    inv = (np.float32(10000.0) ** (-np.arange(16, dtype=np.float32) / np.float32(16))).astype(np.float32)
    ang = np.concatenate([row[:, None] * inv, col[:, None] * inv], axis=-1).astype(np.float32)
    cos = np.cos(ang).astype(np.float32)
    sin = np.sin(ang).astype(np.float32)
    idx = np.arange(128) % 32
    cosT = np.ascontiguousarray(cos[:, idx].T)
    sinT = np.ascontiguousarray(sin[:, idx].T)
    return cosT, sinT


def _na_tables(rpb):
    kp = np.arange(128) // 64
    kc = np.arange(128) % 64
    cs = np.clip(kc - 8, 0, 48)
    colok = (kc[:, None] >= cs[None, :]) & (kc[:, None] < cs[None, :] + 16)
    dcidx = np.clip(kc[:, None] - kc[None, :], -15, 15) + 15

    def tab(o):
        dr = 2 * o + kp[:, None] - kp[None, :]
        ok = colok & (np.abs(dr) <= 7)
        dridx = np.clip(dr, -7, 7) + 7
        t = rpb[:, dridx, dcidx]
        return np.where(ok[None], t, np.float32(NEG)).astype(np.float32)
    btab = np.zeros((16, 128, 7, 128), np.float32)
    for s, o in enumerate([-2, -1, 0, 1, 2]):
        btab[:, :, s, :] = tab(o)
    bx = np.stack([tab(3), tab(-3)], axis=2)
    return btab.reshape(16 * 128, 7 * 128), np.ascontiguousarray(bx).reshape(16 * 128, 2 * 128)


def _na_mrow(p):
    M = np.zeros((2, 16, 8, 2), np.float32)
    for lm in range(16):
        m = lm + 16 * p
        for s in range(8):
            if s < 5:
                o = s - 2
            elif s in (5, 6):
                continue
            else:
                o = 3 if lm == 0 else (-3 if lm == 15 else None)
                if o is None:
                    continue
            for kp in range(2):
                for qp in range(2):
                    kr = 2 * (m + o) + kp
                    qr = 2 * m + qp
                    rs = min(max(qr - 4, 0), 56)
                    ok = (0 <= kr <= 63) and (rs <= kr <= rs + 7)
                    M[kp, lm, s, qp] = 0.0 if ok else NEGM
    return M.reshape(2, 256)


def _swa_tables(p):
    a = np.arange(128)[:, None]
    b = np.arange(128)[None, :]
    prev = np.where(a >= b, 0.0, NEG).astype(np.float32)
    nxt = np.where(a <= b, 0.0, NEG).astype(np.float32)
    allneg = np.full((128, 128), NEG, np.float32)
    var = [prev, nxt, allneg if p == 0 else prev, nxt if p == 0 else allneg]
    t = np.stack([np.tile(v, (1, 4)) for v in var], axis=1)
    return np.ascontiguousarray(t).reshape(128, 4 * 512)


_PROG_CACHE = {}


def _get_prog(layer_ids):
    key = tuple(layer_ids)
    if key not in _PROG_CACHE:
        _PROG_CACHE[key] = build(list(layer_ids))
    return _PROG_CACHE[key][0]


def _layer_inputs(i, inp):
    kind, j = i % 3, i // 3
    f32 = np.float32
    d = {}
    d[f"wmod{i}"] = _chunk_rows(inp["w_mod"][i])
    d[f"bmod{i}"] = np.ascontiguousarray(inp["b_mod"][i].reshape(48, 128).T)
    d[f"gat{i}"] = np.ascontiguousarray(inp["g_attn"][i].reshape(8, 128).T)
    d[f"gff{i}"] = np.ascontiguousarray(inp["g_ffn"][i].reshape(8, 128).T)
    if kind == 0:
        w, wo, gq, gk = inp["na_w_qkv"][j], inp["na_w_o"][j], inp["na_g_q"][j], inp["na_g_k"][j]
        wq, wk, wv = w[:, :1024], w[:, 1024:2048], w[:, 2048:]
    else:
        pre = "swa" if kind == 1 else "ga"
        w, wo, gq, gk = inp[pre + "_w_qkv"][j], inp[pre + "_w_o"][j], inp[pre + "_g_q"][j], inp[pre + "_g_k"][j]
        wq0, wk, wv = w[:, :1024], w[:, 1024:1280], w[:, 1280:]
        cols = []
        for cq in range(8):
            for hf in range(2):
                g = 2 * (cq // 4) + hf
                h = 4 * g + cq % 4
                cols.extend(range(h * 64, h * 64 + 64))
        wq = wq0[:, cols]
    d[f"wq{i}"] = _chunk_rows(wq)
    d[f"wk{i}"] = _chunk_rows(wk)
    d[f"wv{i}"] = _chunk_rows(wv)
    d[f"wo{i}"] = _chunk_rows(wo)
    d[f"gqk{i}"] = np.ascontiguousarray(np.stack([np.tile(gq, 2), np.tile(gk, 2)], axis=1).astype(f32))
    if kind == 0:
        bt, bx = _na_tables(inp["na_rpb"][j])
        d[f"btab{i}"] = bt
        d[f"bx{i}"] = bx
    if kind == 1:
        d[f"sink{i}"] = np.ascontiguousarray(inp["swa_sink"][j].reshape(1, 16))
    wup = inp["ffn_w_up"][i]
    d[f"wup{i}"] = np.ascontiguousarray(
        wup.reshape(8, 128, 2, NF, 128).transpose(3, 1, 2, 0, 4).reshape(NF * 128, 2 * 8 * 128))
    cw = np.concatenate([inp["ffn_conv_w"][i], inp["ffn_conv_b"][i][None]], axis=0)
    d[f"cw{i}"] = np.ascontiguousarray(cw.reshape(4, 44, 128).transpose(2, 1, 0).reshape(128, 44 * 4))
    wdn = inp["ffn_w_down"][i]
    d[f"wdn{i}"] = np.ascontiguousarray(
        wdn.reshape(NF, 128, 8, 128).transpose(2, 1, 0, 3).reshape(8 * 128, NF * 128))
    return d


def _core_inputs(r, layer_ids, inp, shared, xin_r):
    b, p = r // 2, r % 2
    m = {"xin": xin_r}
    m["cin"] = np.ascontiguousarray(
        np.stack([inp["c"][b].reshape(8, 128).T, inp["c_ctx"].reshape(8, 128).T], axis=2).reshape(128, 16))
    m["cmat"] = shared["cmat"]
    m["e2"] = shared["e2"]
    m["hval"] = np.ascontiguousarray(np.tile(np.array([[float(p == 1), float(p == 0)]], np.float32), (128, 1)))
    cosT, sinT = shared["rope"]
    own = slice(p * 2048, p * 2048 + 2048)
    pre0 = max(p * 2048 - 256, 0)
    post0 = min(p * 2048 + 2048, 4096 - 256)
    m["cosd"] = np.ascontiguousarray(np.concatenate([cosT[:, own], cosT[:, pre0:pre0 + 256], cosT[:, post0:post0 + 256], cosT], axis=1))
    m["sind"] = np.ascontiguousarray(np.concatenate([sinT[:, own], sinT[:, pre0:pre0 + 256], sinT[:, post0:post0 + 256], sinT], axis=1))
    for i in layer_ids:
        m.update(shared["layers"][i])
        if i % 3 == 0:
            m[f"mrow{i}"] = shared["mrow"][p]
        if i % 3 == 1:
            m[f"swab{i}"] = shared["swab"][p]
    return m


def _shared(layer_ids, inp):
    sh = {}
    ones = np.ones((128, 128), np.float32)
    bd = np.zeros((128, 128), np.float32)
    bd[:64, :64] = 1.0
    bd[64:, 64:] = 1.0
    rot = np.zeros((128, 128), np.float32)
    for m_ in range(128):
        i = m_ % 64
        if i < 32:
            rot[m_ + 32, m_] = -1.0
        else:
            rot[m_ - 32, m_] = 1.0
    sh["cmat"] = np.ascontiguousarray(np.stack([ones, bd, rot], axis=1).reshape(128, 384))
    e2 = np.zeros((2, 128), np.float32)
    e2[0, :64] = 1.0
    e2[1, 64:] = 1.0
    sh["e2"] = e2
    sh["rope"] = _rope_tables()
    sh["layers"] = {i: _layer_inputs(i, inp) for i in layer_ids}
    sh["mrow"] = [_na_mrow(0), _na_mrow(1)]
    sh["swab"] = [_swa_tables(0), _swa_tables(1)]
    return sh


def _x_layout(inp):
    xs = []
    for r in range(8):
        b, p = r // 2, r % 2
        xa = np.concatenate([inp["x"][b, p * 2048:(p + 1) * 2048], inp["ctx"][b]], axis=0)
        xs.append(np.ascontiguousarray(xa.T.reshape(8, 128, NTOK).transpose(1, 0, 2).reshape(128, 8 * NTOK)))
    return xs


LAUNCH_PLAN = [[0, 1, 2, 3]]


def kernel(**inputs):
    inp = {k: np.asarray(v, dtype=np.float32) for k, v in inputs.items()}
    xs = _x_layout(inp)
    for layer_ids in LAUNCH_PLAN:
        nc = _get_prog(layer_ids)
        sh = _shared(layer_ids, inp)
        in_maps = [_core_inputs(r, layer_ids, inp, sh, xs[r]) for r in range(8)]
        res = run_bass_kernel_spmd(nc, in_maps, core_ids=list(range(8)))
        xs = [np.asarray(res.results[r]["yout"]) for r in range(8)]
    out = np.empty((4, 4096, 1024), np.float32)
    for r in range(8):
        b, p = r // 2, r % 2
        y = xs[r].reshape(128, 8, NTOK)[:, :, :NLAT]
        out[b, p * 2048:(p + 1) * 2048] = y.transpose(2, 1, 0).reshape(NLAT, 1024)
    return out
```

```python
import numpy as np
from collections import deque
from contextlib import ExitStack
import concourse.bass as bass
import concourse.mybir as mybir
from concourse.bass_utils import run_bass_kernel_spmd

F32 = mybir.dt.float32
BF16 = mybir.dt.bfloat16
AF = mybir.ActivationFunctionType
ALU = mybir.AluOpType

D = 1024
NTOK = 2304
NLAT = 2048
NCTX = 256
DFF = 2816
NF = 22
NEG = -30000.0
NEGM = -240000.0
EPS = 1e-6
PAIRS = [[0, 1], [2, 3], [4, 5], [6, 7]]
ARENA_F32 = 33536
SAME_ENGINE_WAITS = True


def _prod(s):
    r = 1
    for a in s:
        r *= a
    return r


class V:
    __slots__ = ("ap", "k")

    def __init__(self, ap, k):
        self.ap = ap
        self.k = k


class Buf:
    def __init__(self, tens, off32, shape, dtype, pref, cell=2048):
        self.esz = 2 if dtype == BF16 else 4
        self.shape = list(shape)
        n = _prod(shape)
        n32 = (n * self.esz + 3) // 4
        self.off_b = off32 * 4
        self.n32 = n32
        ap = tens[:, off32:off32 + n32]
        if dtype == BF16:
            ap = ap.bitcast(BF16)[:, 0:n]
        if len(shape) == 2:
            ap = ap.rearrange("p (a b) -> p a b", a=shape[0])
        elif len(shape) == 3:
            ap = ap.rearrange("p (a b c) -> p a b c", a=shape[0], b=shape[1])
        self.ap = ap
        self.pref = pref
        self.cell = cell
        st = []
        acc = 1
        for s in reversed(shape):
            st.append(acc)
            acc *= s
        self.strides = list(reversed(st))

    def __getitem__(self, idx):
        if not isinstance(idx, tuple):
            idx = (idx,)
        sub = self.ap[idx]
        lo = 0
        hi = 0
        fidx = list(idx[1:]) + [slice(None)] * (len(self.shape) - len(idx) + 1)
        for d, ix in enumerate(fidx):
            if isinstance(ix, slice):
                a = 0 if ix.start is None else ix.start
                b = self.shape[d] if ix.stop is None else ix.stop
            else:
                a, b = ix, ix + 1
            lo += a * self.strides[d]
            hi += (b - 1) * self.strides[d]
        c0 = (self.off_b + lo * self.esz) // self.cell
        c1 = (self.off_b + hi * self.esz) // self.cell
        return V(sub, [(self.pref, c) for c in range(c0, c1 + 1)])

    def all(self):
        return self[(slice(None),)]


class Prog:
    ENG = ("pe", "act", "dve", "pool", "sp")

    def __init__(self, nc, es):
        self.nc = nc
        self.es = es
        self.ops = {e: [] for e in self.ENG}
        self.cnt = {e: 0 for e in self.ENG}
        self.sem = {e: es.enter_context(nc.semaphore("s_" + e)) for e in self.ENG}
        self.seen = {e: {} for e in self.ENG}
        self.lastw = {}
        self.readers = {}
        self.dsem = {}
        self.nwaits = 0

    def _deps(self, reads, writes):
        d = []
        for k in reads:
            w = self.lastw.get(k)
            if w is not None:
                d.append(w)
        for k in writes:
            w = self.lastw.get(k)
            if w is not None:
                d.append(w)
            d.extend(self.readers.get(k, ()))
        return d

    def _waits(self, eng, deps):
        best = {}
        for (sname, sem, val) in deps:
            if sname == "s_" + eng and (eng == "pe" or not SAME_ENGINE_WAITS):
                continue
            if val > best.get(sname, (None, 0))[1]:
                best[sname] = (sem, val)
        for sname, (sem, val) in best.items():
            if self.seen[eng].get(sname, 0) >= val:
                continue
            self.seen[eng][sname] = val
            self.nwaits += 1
            self.ops[eng].append(lambda e, sem=sem, val=val: e.wait_ge(sem, val))

    def _record(self, ev, reads, writes):
        for k in reads:
            self.readers.setdefault(k, []).append(ev)
        for k in writes:
            self.lastw[k] = ev
            self.readers[k] = []

    def op(self, eng, fn, reads=(), writes=()):
        self._waits(eng, self._deps(reads, writes))
        self.cnt[eng] += 1
        sem = self.sem[eng]
        self.ops[eng].append(lambda e, fn=fn, sem=sem: fn(e).then_inc(sem, 1))
        self._record(("s_" + eng, sem, self.cnt[eng]), reads, writes)

    def dma(self, out, in_, semname, eng="sp"):
        reads, writes = in_.k, out.k
        self._waits(eng, self._deps(reads, writes))
        if semname not in self.dsem:
            self.dsem[semname] = [self.es.enter_context(self.nc.semaphore("d_" + semname)), 0]
        s = self.dsem[semname]
        if s[1] > 0:
            self._waits(eng, [("d_" + semname, s[0], s[1])])
        s[1] += 16
        sem = s[0]
        o, i = out.ap, in_.ap
        self.ops[eng].append(lambda e, o=o, i=i, sem=sem: e.dma_start(out=o, in_=i).then_inc(sem, 16))
        self._record(("d_" + semname, sem, s[1]), reads, writes)

    def collective(self, ins, outs, semname):
        reads, writes = ins.k, outs.k
        self._waits("pool", self._deps(reads, writes))
        if semname not in self.dsem:
            self.dsem[semname] = [self.es.enter_context(self.nc.semaphore("d_" + semname)), 0]
        s = self.dsem[semname]
        if s[1] > 0:
            self._waits("pool", [("d_" + semname, s[0], s[1])])
        s[1] += 1
        sem = s[0]
        i, o = ins.ap, outs.ap
        self.ops["pool"].append(lambda e, i=i, o=o, sem=sem: e.collective_compute(
            "AllGather", ALU.bypass, replica_groups=PAIRS, ins=[i], outs=[o]).then_inc(sem))
        self._record(("d_" + semname, sem, s[1]), reads, writes)

    def mm(self, out, lhsT, rhs, start=True, stop=True):
        self.op("pe", lambda e: e.matmul(out.ap, lhsT.ap, rhs.ap, start=start, stop=stop),
                reads=lhsT.k + rhs.k, writes=out.k)

    def act(self, out, in_, func, scale=1.0, bias=0.0, eng="act"):
        r = list(in_.k)
        sc = scale
        bi = bias
        if isinstance(scale, V):
            r += scale.k
            sc = scale.ap
        if isinstance(bias, V):
            r += bias.k
            bi = bias.ap
        self.op("act", lambda e: e.activation(out=out.ap, in_=in_.ap, func=func, bias=bi, scale=sc),
                reads=r, writes=out.k)

    def tt(self, eng, out, a, b, op):
        self.op(eng, lambda e: e.tensor_tensor(out=out.ap, in0=a.ap, in1=b.ap, op=op),
                reads=a.k + b.k, writes=out.k)

    def ts(self, eng, out, a, s1, s2, op0, op1=None):
        r = list(a.k)
        x1, x2 = s1, s2
        if isinstance(s1, V):
            r += s1.k
            x1 = s1.ap
        if isinstance(s2, V):
            r += s2.k
            x2 = s2.ap
        if op1 is None:
            self.op(eng, lambda e: e.tensor_scalar(out.ap, a.ap, x1, None, op0), reads=r, writes=out.k)
        else:
            self.op(eng, lambda e: e.tensor_scalar(out.ap, a.ap, x1, x2, op0, op1), reads=r, writes=out.k)

    def stt(self, out, a, s, b, op0, op1):
        r = a.k + b.k
        x = s
        if isinstance(s, V):
            r = r + s.k
            x = s.ap
        self.op("dve", lambda e: e.scalar_tensor_tensor(out=out.ap, in0=a.ap, scalar=x, in1=b.ap, op0=op0, op1=op1),
                reads=r, writes=out.k)

    def copy(self, eng, out, a):
        if eng == "act":
            self.op("act", lambda e: e.copy(out.ap, a.ap), reads=a.k, writes=out.k)
        else:
            self.op(eng, lambda e: e.tensor_copy(out.ap, a.ap), reads=a.k, writes=out.k)

    def recip(self, out, a):
        self.op("dve", lambda e: e.reciprocal(out.ap, a.ap), reads=a.k, writes=out.k)

    def memset(self, eng, out, val):
        self.op(eng, lambda e: e.memset(out.ap, val), reads=(), writes=out.k)

    def emit(self, final_waits):
        nc = self.nc
        engmap = {"pe": "tensor", "act": "scalar", "dve": "vector", "pool": "gpsimd", "sp": "sync"}
        for (sname, sem, val) in final_waits:
            self.ops["sp"].append(lambda e, sem=sem, val=val: e.wait_ge(sem, val))
        with nc.Block() as block:
            for en in self.ENG:
                ops = self.ops[en]

                def body(e, ops=ops):
                    for f in ops:
                        f(e)
                getattr(block, engmap[en])(body)


class Ring:
    def __init__(self, bufs):
        self.bufs = bufs
        self.i = 0

    def next(self):
        b = self.bufs[self.i % len(self.bufs)]
        self.i += 1
        return b


def build(layer_ids, stage="full"):
    nc = bass.Bass("TRN2", target_bir_lowering=False)
    es = ExitStack()
    P = Prog(nc, es)

    def dram_in(name, shape, dt=F32):
        return nc.dram_tensor(name, list(shape), dt, kind="ExternalInput")

    xin = dram_in("xin", [128, 8 * NTOK])
    cin = dram_in("cin", [128, 16])
    cmat_d = dram_in("cmat", [128, 3 * 128])
    e2_d = dram_in("e2", [2, 128])
    hval_d = dram_in("hval", [128, 2])
    cos_d = dram_in("cosd", [128, NLAT + 512 + 4096])
    sin_d = dram_in("sind", [128, NLAT + 512 + 4096])
    yout = nc.dram_tensor("yout", [128, 8 * NTOK], F32, kind="ExternalOutput")
    L = {}
    for i in layer_ids:
        kind = i % 3
        d = {}
        d["wmod"] = dram_in(f"wmod{i}", [128, 8 * 6144])
        d["bmod"] = dram_in(f"bmod{i}", [128, 48])
        d["gat"] = dram_in(f"gat{i}", [128, 8])
        d["gff"] = dram_in(f"gff{i}", [128, 8])
        kvw = 1024 if kind == 0 else 256
        d["wq"] = dram_in(f"wq{i}", [128, 8 * 1024])
        d["wk"] = dram_in(f"wk{i}", [128, 8 * kvw])
        d["wv"] = dram_in(f"wv{i}", [128, 8 * kvw])
        d["wo"] = dram_in(f"wo{i}", [128, 8 * 1024])
        d["gqk"] = dram_in(f"gqk{i}", [128, 2])
        if kind == 0:
            d["btab"] = dram_in(f"btab{i}", [16 * 128, 7 * 128])
            d["bx"] = dram_in(f"bx{i}", [16 * 128, 2 * 128])
            d["mrow"] = dram_in(f"mrow{i}", [2, 16 * 8 * 2])
        if kind == 1:
            d["swab"] = dram_in(f"swab{i}", [128, 4 * 512])
            d["sink"] = dram_in(f"sink{i}", [1, 16])
        d["wup"] = dram_in(f"wup{i}", [NF * 128, 2 * 8 * 128])
        d["cw"] = dram_in(f"cw{i}", [128, 44 * 4])
        d["wdn"] = dram_in(f"wdn{i}", [8 * 128, NF * 128])
        L[i] = d

    hown = nc.dram_tensor("hown", [4 * 128, 8 * 512], BF16)
    hctx = nc.dram_tensor("hctx", [128, 8 * NCTX], BF16)
    snd = nc.dram_tensor("snd", [2 * 128, 8 * 256], BF16)
    rcv = nc.dram_tensor("rcv", [4 * 128, 8 * 256], BF16)
    rcvg = nc.dram_tensor("rcvg", [4 * 2 * 128, 8 * 512], BF16)
    snd2 = nc.dram_tensor("snd2", [2, 1024], BF16)
    rcv2 = nc.dram_tensor("rcv2", [4, 1024], BF16)

    def dview(t, key):
        return V(t, [key])

    xT_t = es.enter_context(nc.sbuf_tensor("xT", [128, 8 * NTOK], F32))
    ar_t = es.enter_context(nc.sbuf_tensor("arena", [128, ARENA_F32], F32))
    cs_t = es.enter_context(nc.sbuf_tensor("consts", [128, 768], F32))
    ps_t = es.enter_context(nc.psum_tensor("ps", [128, 8 * 512], F32))
    xT = Buf(xT_t, 0, [8, NTOK], F32, "x")
    PS = Buf(ps_t, 0, [8, 512], F32, "ps")
    cmat = Buf(cs_t, 0, [3, 128], F32, "cs", cell=64)
    csil = Buf(cs_t, 384, [8, 2], F32, "cs", cell=64)
    hval = Buf(cs_t, 400, [2], F32, "cs", cell=64)
    e2b = Buf(cs_t, 402, [128], BF16, "cs", cell=64)
    gqk = Buf(cs_t, 466, [2], F32, "cs", cell=64)
    sinkt = Buf(cs_t, 468, [16], F32, "cs", cell=64)
    modb = Buf(cs_t, 484, [48, 2], F32, "cs", cell=64)
    gs = Buf(cs_t, 580, [2, 8, 2], F32, "cs", cell=64)
    gtmp = Buf(cs_t, 612, [2, 8], F32, "cs", cell=64)
    bmodt = Buf(cs_t, 628, [48], F32, "cs", cell=64)
    halo = Buf(cs_t, 676, [8, 2], BF16, "cs", cell=64)
    halor = Buf(cs_t, 684, [2, 8], BF16, "cs", cell=64)
    bnd = Buf(cs_t, 692, [2, 8], BF16, "cs", cell=64)
    h2sv = Buf(cs_t, 704, [8, 1], BF16, "cs", cell=64)
    csilb = Buf(cs_t, 712, [8, 2], BF16, "cs", cell=64)
    ones_v = cmat[:, 0, :]
    bdiag_v = cmat[:, 1, :]
    rot_v = cmat[:, 2, :]

    class Arena:
        def __init__(self):
            self.off = 0

        def reset(self, off=0):
            self.off = (off + 255) // 256 * 256

        def alloc(self, shape, dtype):
            n32 = (_prod(shape) * (2 if dtype == BF16 else 4) + 3) // 4
            n32 = (n32 + 255) // 256 * 256
            b = Buf(ar_t, self.off, shape, dtype, "ar", cell=1024)
            self.off += n32
            assert self.off <= ARENA_F32, f"arena overflow {self.off}"
            return b

    AR = Arena()

    for c_ in range(8):
        P.dma(xT[:, c_, :], V(xin[:, c_ * NTOK:(c_ + 1) * NTOK], []), f"ldx{c_ % 4}", eng=("sp" if c_ % 2 == 0 else "act"))
    P.dma(cmat.all(), V(cmat_d[:, :].rearrange("p (a b) -> p a b", a=3), []), "ldc")
    P.dma(csil.all(), V(cin[:, :].rearrange("p (a b) -> p a b", a=8), []), "ldc")
    P.dma(hval.all(), V(hval_d[:, :], []), "ldc")
    P.memset("dve", e2b.all(), 0.0)
    P.dma(e2b[0:2, :], V(e2_d[:, :], []), "ldp", eng="pool")
    P.act(csil.all(), csil.all(), AF.Silu)
    P.copy("dve", csilb.all(), csil.all())

    def norm_cols(tok0, n, which, col, out_fn, scr):
        sqr, tbr, lnt, rstd, stat = scr
        stat_v = stat[:, 0:n]
        for c in range(8):
            xs = xT[:, c, tok0:tok0 + n]
            sq = sqr.next()[:, 0:n]
            P.tt("dve", sq, xs, xs, ALU.mult)
            P.mm(stat_v, ones_v, sq, start=(c == 0), stop=(c == 7))
        ln_v = lnt[:, 0:n]
        rs_v = rstd[:, 0:n]
        P.act(ln_v, stat_v, AF.Ln, scale=1.0 / D, bias=EPS_V)
        P.act(rs_v, ln_v, AF.Exp, scale=-0.5)
        sh = 0 if which == 0 else 3
        for c in range(8):
            xs = xT[:, c, tok0:tok0 + n]
            t = tbr.next()[:, 0:n]
            P.tt("dve", t, xs, rs_v, ALU.mult)
            P.act(out_fn(c), t, AF.Identity, scale=gs[:, which, c, col:col + 1], bias=modb[:, sh * 8 + c, col:col + 1])

    epsb = Buf(cs_t, 700, [1], F32, "cs", cell=64)
    P.memset("dve", epsb.all(), EPS)
    EPS_V = epsb[:, 0:1]

    def emit_mod(i):
        d = L[i]
        AR.reset()
        wmr = Ring([AR.alloc([8, 512], F32) for _ in range(4)])
        wmbr = Ring([AR.alloc([8, 512], BF16) for _ in range(2)])
        P.dma(bmodt.all(), V(d["bmod"][:, :], []), "ldc")
        P.dma(gtmp[:, 0, :], V(d["gat"][:, :], []), "ldc")
        P.dma(gtmp[:, 1, :], V(d["gff"][:, :], []), "ldc")
        wsrc = d["wmod"][:, :].rearrange("p (k n) -> p k n", k=8)
        for pc in range(12):
            wm = wmr.next()
            P.dma(wm.all(), V(wsrc[:, :, pc * 512:(pc + 1) * 512], []), f"wm{pc % 4}", eng=("sp" if pc % 2 == 0 else "act"))
            wmb = wmbr.next()
            P.copy("act" if pc % 2 == 0 else "dve", wmb.all(), wm.all())
            for jj in range(4):
                j = pc * 4 + jj
                o = PS[:, 0, 2 * j:2 * j + 2]
                for k in range(8):
                    P.mm(o, wmb[:, k, jj * 128:(jj + 1) * 128], csilb[:, k, :], start=(k == 0), stop=(k == 7))
        pv = PS[:, 0, 0:96]
        P.tt("dve", V(modb.ap, modb.all().k),
             V(pv.ap.rearrange("p (j t) -> p j t", t=2), pv.k),
             V(bmodt.ap.unsqueeze(2).broadcast_to([128, 48, 2]), bmodt.all().k), ALU.add)
        for w_ in range(2):
            scv = modb[:, (1 + 3 * w_) * 8:(2 + 3 * w_) * 8, :]
            P.ts("dve", gs[:, w_, :, :], scv, 1.0, None, ALU.add)
            P.tt("dve", gs[:, w_, :, :], gs[:, w_, :, :],
                 V(gtmp.ap[:, w_, :].unsqueeze(2).broadcast_to([128, 8, 2]), gtmp.all().k), ALU.mult)

    def emit_attention(i):
        d = L[i]
        kind = i % 3
        last = (i == 3)
        is_na = (kind == 0)
        rope = (kind != 0)
        npass = 4 if is_na else 2
        NKC = 34 if kind == 2 else 22
        nQ = 2 if is_na else 4
        NH = 4 if is_na else 2
        VW = NH * 64
        P.dma(gqk.all(), V(d["gqk"][:, :], []), "ldc")
        if kind == 1:
            P.dma(sinkt[64:65, :], V(d["sink"][:, :], []), "ldc")
            P.act(sinkt[64:65, :], sinkt[64:65, :], AF.Exp)

        AR.reset()
        sqr = Ring([AR.alloc([512], F32) for _ in range(3)])
        tbr = Ring([AR.alloc([512], F32) for _ in range(3)])
        lnt = AR.alloc([512], F32)
        rstd = AR.alloc([512], F32)
        hst = Ring([AR.alloc([8, 512], BF16) for _ in range(2)])
        scr = (sqr, tbr, lnt, rstd, PS[:, 0, :])
        scr = (sqr, tbr, lnt, rstd, Buf(ps_t, 0, [512], F32, "ps"))
        def hown_blk(tb):
            return hown[tb * 128:(tb + 1) * 128, :].rearrange("p (c t) -> p c t", c=8)

        def rcvg_blk(tb, r):
            return rcvg[(tb * 2 + r) * 128:(tb * 2 + r + 1) * 128, :].rearrange("p (c t) -> p c t", c=8)
        hctx_v = hctx[:, :].rearrange("p (c t) -> p c t", c=8)
        snd_v = snd[:, :].rearrange("(s p) (c t) -> s p c t", s=2, c=8)
        for tb in range(5):
            n = 512 if tb < 4 else 256
            hs = hst.next()
            norm_cols(tb * 512, n, 0, 0 if tb < 4 else 1, lambda c, hs=hs, n=n: hs[:, c, 0:n], scr)
            if tb < 4:
                P.dma(V(hown_blk(tb), [("hown", tb)]), hs.all(), f"hst{tb % 2}")
                if kind == 2:
                    P.collective(V(hown[tb * 128:(tb + 1) * 128, :], [("hown", tb)]),
                                 V(rcvg[tb * 256:(tb + 1) * 256, :], [("rcvg", tb)]), "cc")
                if kind != 2 and tb == 0:
                    P.dma(V(snd_v[0], [("snd", 0)]), hs[:, :, 0:256], f"hst{tb % 2}")
                if kind != 2 and tb == 3:
                    P.dma(V(snd_v[1], [("snd", 1)]), hs[:, :, 256:512], f"hst{tb % 2}")
            else:
                P.dma(V(hctx_v, [("hctx", 0)]), hs[:, :, 0:256], f"hst{tb % 2}")
        if kind != 2:
            P.collective(V(snd[:, :], [("snd", 0), ("snd", 1)]), V(rcv[:, :], [("rcv", 0)]), "cc")
        rcv_v = rcv[:, :].rearrange("(r s p) (c t) -> r s p c t", r=2, s=2, c=8)

        blocks = []
        if kind != 2:
            blocks.append((V(rcv_v[0, 1], [("rcv", 0)]), 256, 0, None, NLAT if rope else None))
            for tb in range(4):
                blocks.append((V(hown_blk(tb), [("hown", tb)]), 512, 2 + 4 * tb, tb * 512,
                               tb * 512 if rope else None))
            blocks.append((V(rcv_v[1, 0], [("rcv", 0)]), 256, 18, None, NLAT + 256 if rope else None))
            blocks.append((V(hctx_v, [("hctx", 0)]), 256, 20, None if last else 2048, None))
        else:
            for r in range(2):
                for tb in range(4):
                    kc0_ = 16 * r + 4 * tb
                    blocks.append((V(rcvg_blk(tb, r), [("rcvg", tb)]), 512, kc0_, None,
                                   GA_COS_OFF + kc0_ * 128))
            for tb in range(4):
                blocks.append((V(hown_blk(tb), [("hown", tb)]), 512, None, tb * 512, tb * 512))
            blocks.append((V(hctx_v, [("hctx", 0)]), 256, 32, 2048, None))

        wq_v = d["wq"][:, :].rearrange("p (k n) -> p k n", k=8)
        wk_v = d["wk"][:, :].rearrange("p (k n) -> p k n", k=8)
        wv_v = d["wv"][:, :].rearrange("p (k n) -> p k n", k=8)
        wo_v = d["wo"][:, :].rearrange("p (k n) -> p k n", k=8)

        for pz in range(npass):
            AR.reset()
            KT = AR.alloc([NH, NKC * 128], BF16)
            VT = AR.alloc([NKC, NH, 65], BF16)
            QT = AR.alloc([nQ, NTOK], BF16)
            persist_off = AR.off
            P.memset("dve", VT[:, :, :, 64:65], 1.0)
            for h_ in range(NH):
                for c0_ in range(0, NKC * 128, 2048):
                    P.memset("dve", KT[:, h_, c0_:min(c0_ + 2048, NKC * 128)], 0.0)
            wqr = Ring([AR.alloc([8, 128], BF16) for _ in range(2)])
            wkr = Ring([AR.alloc([8, 128], BF16) for _ in range(2)])
            wvb = AR.alloc([8, 256], BF16)
            hbr = Ring([AR.alloc([8, 512], BF16) for _ in range(2)])
            qcr = Ring([AR.alloc([512], F32) for _ in range(2)])
            nr_ = 2
            sq2 = Ring([AR.alloc([512], F32) for _ in range(2)])
            ln2 = AR.alloc([512], F32)
            rs2 = Ring([AR.alloc([512], F32) for _ in range(nr_)])
            qnr = Ring([AR.alloc([512], F32) for _ in range(2)])
            t1r = Ring([AR.alloc([512], F32) for _ in range(nr_)])
            t2r = Ring([AR.alloc([512], F32) for _ in range(nr_)])
            csr = Ring([AR.alloc([2, 512], F32) for _ in range(4)])
            P.dma(wvb[:, :, 0:VW], V(wv_v[:, :, pz * VW:(pz + 1) * VW], []), "wvb", eng="pool")
            projps = Ring([Buf(ps_t, 512 * b, [512], F32, "ps") for b in (1, 2)])
            st2ps = Buf(ps_t, 512 * 3, [512], F32, "ps")
            ropeps = Buf(ps_t, 512 * 4, [512], F32, "ps")
            vps = Ring([Buf(ps_t, 512 * b, [512], F32, "ps") for b in (5, 6)])

            pendB = deque()
            pendC = deque()

            def post_A(ps, n, gcol, dst, cs):
                qc = qcr.next()[:, 0:n]
                P.copy("act", qc, ps[:, 0:n])
                sq = sq2.next()[:, 0:n]
                P.tt("dve", sq, qc, qc, ALU.mult)
                return (n, gcol, dst, cs, qc, sq)

            def post_B(task):
                n, gcol, dst, cs, qc, sq = task
                P.mm(st2ps[:, 0:n], bdiag_v, sq)
                lv = ln2[:, 0:n]
                P.act(lv, st2ps[:, 0:n], AF.Ln, scale=1.0 / 64, bias=EPS_V)
                rs = rs2.next()[:, 0:n]
                P.act(rs, lv, AF.Exp, scale=-0.5)
                if cs is None:
                    for (psl, dv) in dst:
                        P.stt(dv, V(qc.ap[psl], qc.k), gqk[psl, gcol:gcol + 1], V(rs.ap[psl], rs.k), ALU.mult, ALU.mult)
                    return None
                qn = qnr.next()[:, 0:n]
                P.stt(qn, qc, gqk[:, gcol:gcol + 1], rs, ALU.mult, ALU.mult)
                return (n, dst, cs, qn)

            def post_C(task):
                n, dst, cs, qn = task
                P.mm(ropeps[:, 0:n], rot_v, qn)
                t1 = t1r.next()[:, 0:n]
                P.tt("dve", t1, qn, cs[:, 0, 0:n], ALU.mult)
                t2 = t2r.next()[:, 0:n]
                P.tt("dve", t2, ropeps[:, 0:n], cs[:, 1, 0:n], ALU.mult)
                for (psl, dv) in dst:
                    P.tt("dve", dv, V(t1.ap[psl], t1.k), V(t2.ap[psl], t2.k), ALU.add)

            def post_tick(newtask):
                if pendC:
                    post_C(pendC.popleft())
                if pendB:
                    r_ = post_B(pendB.popleft())
                    if r_ is not None:
                        pendC.append(r_)
                if newtask is not None:
                    pendB.append(newtask)

            def qk_post(ps, n, gcol, dst, cs):
                post_tick(post_A(ps, n, gcol, dst, cs))

            if is_na:
                qcols = [pz * 256 + c * 128 for c in range(2)]
                kcols = [pz * 256 + c * 128 for c in range(2)]
            else:
                qcols = [(4 * pz + c) * 128 for c in range(4)]
                kcols = [pz * 128]

            for (src, n, kc0, qt0, cscol) in blocks:
                hb = hbr.next()
                P.dma(hb[:, :, 0:n], src, f"hb{(hbr.i - 1) % 2}")
                cs = None
                if cscol is not None:
                    cs = csr.next()
                    P.dma(cs[:, 0, 0:n], V(cos_d[:, cscol:cscol + n], []), f"cs{(csr.i - 1) % 4}")
                    P.dma(cs[:, 1, 0:n], V(sin_d[:, cscol:cscol + n], []), f"cs{(csr.i - 1) % 4}")
                if qt0 is not None:
                    for ci, qc_ in enumerate(qcols):
                        wq = wqr.next()
                        P.dma(wq.all(), V(wq_v[:, :, qc_:qc_ + 128], []), f"wq{(wqr.i - 1) % 2}", eng="pool")
                        pp = projps.next()
                        for k in range(8):
                            P.mm(pp[:, 0:n], wq[:, k, :], hb[:, k, 0:n], start=(k == 0), stop=(k == 7))
                        qk_post(pp, n, 0, [(slice(0, 128), QT[:, ci, qt0:qt0 + n])], cs)
                if kc0 is not None:
                    for ci, kc_ in enumerate(kcols):
                        wk = wkr.next()
                        P.dma(wk.all(), V(wk_v[:, :, kc_:kc_ + 128], []), f"wk{(wkr.i - 1) % 2}", eng="pool")
                        pp = projps.next()
                        for k in range(8):
                            P.mm(pp[:, 0:n], wk[:, k, :], hb[:, k, 0:n], start=(k == 0), stop=(k == 7))
                        qk_post(pp, n, 1, [(slice(0, 64), KT[0:64, 2 * ci, kc0 * 128:kc0 * 128 + n]),
                                           (slice(64, 128), KT[64:128, 2 * ci + 1, kc0 * 128:kc0 * 128 + n])], cs)
                    for ts_ in range(n // 128):
                        vp = vps.next()
                        for k in range(8):
                            P.mm(vp[:, 0:VW], hb[:, k, ts_ * 128:(ts_ + 1) * 128], wvb[:, k, 0:VW], start=(k == 0), stop=(k == 7))
                        P.copy("act", VT[:, kc0 + ts_, :, 0:64],
                               V(vp.ap[:, 0:VW].rearrange("p (h d) -> p h d", h=NH), vp[:, 0:VW].k))

            while pendB or pendC:
                post_tick(None)

            AR.reset(persist_off)
            OT = AR.alloc([8, 512], BF16)
            wo_rows = 2 if is_na else 4
            WO = AR.alloc([wo_rows, 1024], BF16)
            P.dma(WO.all(), V(wo_v[:, pz * wo_rows:(pz + 1) * wo_rows, :], []), "wo", eng="pool")
            tmpr = Ring([AR.alloc([1024], F32) for _ in range(2)])
            if is_na:
                ptr = Ring([AR.alloc([1024], BF16) for _ in range(2)])
            else:
                ptr4 = Ring([AR.alloc([512], BF16) for _ in range(4)])
            rden = AR.alloc([512], F32)
            ounr = Ring([AR.alloc([512], F32) for _ in range(2)])
            if is_na:
                BT = AR.alloc([4, 896], F32)
                BX = AR.alloc([4, 2, 128], F32)
                MR = AR.alloc([16, 8, 2], BF16)
                bt_v = d["btab"][:, :].rearrange("(h p) n -> p h n", p=128)
                bx_v = d["bx"][:, :].rearrange("(h p) (s n) -> p h s n", p=128, s=2)
                P.dma(BT.all(), V(bt_v[:, pz * 4:pz * 4 + 4, :], []), "bt")
                P.dma(BX.all(), V(bx_v[:, pz * 4:pz * 4 + 4], []), "bt")
                P.memset("dve", MR.all(), 0.0)
                P.dma(MR[0:2], V(d["mrow"][:, :].rearrange("p (a b c) -> p a b c", a=16, b=8), []), "mr", eng="pool")
            if kind == 1:
                SB = AR.alloc([4, 512], F32)
                P.dma(SB.all(), V(d["swab"][:, :].rearrange("p (a b) -> p a b", a=4), []), "bt")
            sps = Ring([Buf(ps_t, 512 * b, [1024], F32, "ps") for b in (0, 2)])
            ops_ = Ring([Buf(ps_t, 512 * b, [512], F32, "ps") for b in (4, 5)])
            bcps = Buf(ps_t, 512 * 6, [512], F32, "ps")
            yps = Ring([Buf(ps_t, 512 * b, [512], F32, "ps") for b in (7, 6)])

            def normalize(po, ncols, heads_dst, sinkrow=None):
                dn = po[64:65, 0:ncols]
                rd = rden[64:65, 0:ncols]
                P.act(rd, dn, AF.Ln)
                P.act(rd, rd, AF.Exp, scale=-1.0)
                P.mm(bcps[0:64, 0:ncols], cmat[64:65, 0, 0:64], rd)
                ou = ounr.next()[0:64, 0:ncols]
                P.copy("act", ou, po[0:64, 0:ncols])
                for (c0, chunk, half, t0) in heads_dst:
                    P.tt("dve", OT[half * 64:half * 64 + 64, chunk, t0:t0 + 128],
                         V(ou.ap[:, c0:c0 + 128], ou.k), V(bcps.ap[0:64, c0:c0 + 128], bcps[0:64, 0:ncols].k), ALU.mult)

            def out_proj(tok0, ntok, col):
                for dc in range(8):
                    yp = yps.next()
                    nk = wo_rows
                    for c in range(nk):
                        P.mm(yp[:, 0:ntok], WO[:, c, dc * 128:(dc + 1) * 128], OT[:, (c if is_na else c), 0:ntok],
                             start=(c == 0), stop=(c == nk - 1))
                    xs = xT[:, dc, tok0:tok0 + ntok]
                    P.stt(xs, yp[:, 0:ntok], modb[:, 16 + dc, col:col + 1], xs, ALU.mult, ALU.add)

            qblocks = list(range(16)) + ([] if last else [16, 17])
            pend = deque()

            pend_fin = deque()

            def finish_group(qb, po, heads, use_sink, g, lastg):
                rd = rden[64:65, 0:512]
                if use_sink:
                    sinkrow = V(sinkt.ap[64:65, 4 * g:4 * g + 4].unsqueeze(2).broadcast_to([1, 4, 128]), sinkt[64:65, :].k)
                    dn = V(po.ap[64:65, 0:512].rearrange("p (a b) -> p a b", a=4), po[64:65, 0:512].k)
                    rd3 = V(rden.ap[64:65, 0:512].rearrange("p (a b) -> p a b", a=4), rden[64:65, 0:512].k)
                    P.tt("dve", rd3, dn, sinkrow, ALU.add)
                    P.act(rd, rd, AF.Ln)
                else:
                    P.act(rd, po[64:65, 0:512], AF.Ln)
                P.act(rd, rd, AF.Exp, scale=-1.0)
                ou = ounr.next()[0:64, 0:512]
                P.copy("act", ou, po[0:64, 0:512])

                def part_b():
                    P.mm(bcps[0:64, 0:512], cmat[64:65, 0, 0:64], rd)
                    for (c0, chunk, half, t0) in heads:
                        P.tt("dve", OT[half * 64:half * 64 + 64, chunk, t0:t0 + 128],
                             V(ou.ap[:, c0:c0 + 128], ou.k), V(bcps.ap[0:64, c0:c0 + 128], bcps[0:64, 0:512].k), ALU.mult)
                    if lastg and qb % 4 == 3 and qb < 16:
                        out_proj((qb // 4) * 512, 512, 0)
                    if lastg and qb == 17:
                        out_proj(2048, 256, 1)
                pend_fin.append([0, part_b])

            def fin_tick(flush=False):
                while pend_fin and (flush or pend_fin[0][0] >= 1):
                    pend_fin.popleft()[1]()
                for e_ in pend_fin:
                    e_[0] += 1

            if is_na:
                SK = 1
                state = {"po": None}

                def na_stage1(qb, hl):
                    is_ctxq = qb >= 16
                    lm = qb
                    tq0 = qb * 128
                    chunk, half = hl // 2, hl % 2
                    hs = slice(half * 64, half * 64 + 64)
                    if is_ctxq:
                        slots = [(20, None), (21, None)]
                    else:
                        slots = [(lm + s_, s_) for s_ in range(5)] + [(20, 5), (21, 6)]
                        if lm == 0:
                            slots.append((5, 7))
                        if lm == 15:
                            slots.append((14, 7))
                    nsl = len(slots)
                    sp_ = sps.next()
                    if not is_ctxq:
                        for b0 in range(0, nsl, 4):
                            b1 = min(b0 + 4, nsl)
                            mr = V(MR.ap[:, lm, b0:b1, :].unsqueeze(3).broadcast_to([128, b1 - b0, 2, 64]), MR.all().k)
                            P.mm(sp_[:, b0 * 128:b1 * 128], e2b.all(), mr, start=True, stop=False)
                    for si, (kc, _) in enumerate(slots):
                        P.mm(sp_[:, si * 128:(si + 1) * 128], KT[:, hl, kc * 128:(kc + 1) * 128],
                             QT[:, chunk, tq0:tq0 + 128], start=is_ctxq,
                             stop=(True if is_ctxq else (si == nsl - 1 or si % 4 == 3)))
                    pt = ptr.next()
                    if is_ctxq:
                        P.act(pt[:, 0:nsl * 128], sp_[:, 0:nsl * 128], AF.Exp, scale=0.125)
                    else:
                        tm = tmpr.next()
                        P.stt(tm[:, 0:896], sp_[:, 0:896], 0.125, BT[:, hl, :], ALU.mult, ALU.add)
                        if nsl == 8:
                            P.stt(tm[:, 896:1024], sp_[:, 896:1024], 0.125, BX[:, hl, 0 if lm == 0 else 1, :], ALU.mult, ALU.add)
                        P.act(pt[:, 0:nsl * 128], tm[:, 0:nsl * 128], AF.Exp)
                    return (qb, hl, slots, pt)

                def na_stage2(item):
                    qb, hl, slots, pt = item
                    nsl = len(slots)
                    if hl == 0:
                        state["po"] = ops_.next()
                    po = state["po"]
                    for si, (kc, _) in enumerate(slots):
                        P.mm(po[0:65, hl * 128:(hl + 1) * 128], VT[:, kc, hl, :], pt[:, si * 128:(si + 1) * 128],
                             start=(si == 0), stop=(si == nsl - 1))
                    if hl == 3:
                        otc = (qb % 4) * 128
                        finish_group(qb, po, [(h_ * 128, h_ // 2, h_ % 2, otc) for h_ in range(4)], False, 0, True)

                for qb in qblocks:
                    for hl in range(4):
                        pend.append(na_stage1(qb, hl))
                        fin_tick()
                        if len(pend) > SK:
                            na_stage2(pend.popleft())
                while pend:
                    fin_tick(flush=True)
                    na_stage2(pend.popleft())
                fin_tick(flush=True)
            else:
                SK = 3
                state = {"po": None}
                sps1 = Ring([Buf(ps_t, 512 * b, [512], F32, "ps") for b in (0, 1, 2, 3)])

                def g_stage1(qb, g, si, kc, bvar, nsl):
                    tq0 = qb * 128
                    sp_ = sps1.next()
                    qsl = QT[:, 0:4, tq0:tq0 + 128]
                    P.mm(sp_[:, 0:512], KT[:, g, kc * 128:(kc + 1) * 128], qsl)
                    pt = ptr4.next()
                    if bvar is None:
                        P.act(pt[:, 0:512], sp_[:, 0:512], AF.Exp, scale=0.125)
                    else:
                        tm = tmpr.next()
                        P.stt(tm[:, 0:512], sp_[:, 0:512], 0.125, SB[:, bvar, :], ALU.mult, ALU.add)
                        P.act(pt[:, 0:512], tm[:, 0:512], AF.Exp)
                    return (qb, g, si, kc, nsl, pt)

                def g_stage2(item):
                    qb, g, si, kc, nsl, pt = item
                    if si == 0:
                        state["po"] = ops_.next()
                    po = state["po"]
                    P.mm(po[0:65, 0:512], VT[:, kc, g, :], pt[:, 0:512], start=(si == 0), stop=(si == nsl - 1))
                    if si == nsl - 1:
                        otc = (qb % 4) * 128
                        heads = [(i_ * 128, 2 * g + i_ // 2, i_ % 2, otc) for i_ in range(4)]
                        finish_group(qb, po, heads, kind == 1, 2 * pz + g, g == 1)

                for qb in qblocks:
                    is_ctxq = qb >= 16
                    for g in range(2):
                        if is_ctxq:
                            slots = [(NKC - 2, None), (NKC - 1, None)]
                        elif kind == 1:
                            slots = [(qb + 1, 2 if qb == 0 else 0), (qb + 2, None), (qb + 3, 3 if qb == 15 else 1), (20, None), (21, None)]
                        else:
                            slots = [(kc, None) for kc in range(NKC)]
                        for si, (kc, bvar) in enumerate(slots):
                            pend.append(g_stage1(qb, g, si, kc, bvar, len(slots)))
                            fin_tick()
                            if len(pend) > SK:
                                g_stage2(pend.popleft())
                while pend:
                    fin_tick(flush=True)
                    g_stage2(pend.popleft())
                fin_tick(flush=True)

    def emit_ffn(i):
        d = L[i]
        last = (i == 3)
        AR.reset()
        h2 = AR.alloc([8, 1026], BF16)
        actT = AR.alloc([NF, 1024], BF16)
        wur = Ring([AR.alloc([2, 8, 128], BF16) for _ in range(2)])
        wdr = Ring([AR.alloc([NF, 128], BF16) for _ in range(2)])
        uar = Ring([AR.alloc([1026], F32) for _ in range(2)])
        ugr = Ring([AR.alloc([1026], F32) for _ in range(2)])
        tar = Ring([AR.alloc([1024], F32) for _ in range(2)])
        tgr = Ring([AR.alloc([1024], F32) for _ in range(2)])
        sqr = Ring([AR.alloc([512], F32) for _ in range(2)])
        tbr = Ring([AR.alloc([512], F32) for _ in range(2)])
        lnt = AR.alloc([512], F32)
        rstd = AR.alloc([512], F32)
        CW = AR.alloc([44, 4], F32)
        scr = (sqr, tbr, lnt, rstd, Buf(ps_t, 0, [512], F32, "ps"))
        P.dma(CW.all(), V(d["cw"][:, :].rearrange("p (f t) -> p f t", f=44), []), "ldc")
        wup_v = d["wup"][:, :].rearrange("(f p) (t k n) -> f p t k n", p=128, t=2, k=8)
        wdn_v = d["wdn"][:, :].rearrange("(c p) (f n) -> c p f n", p=128, f=NF)

        norm_cols(0, 1, 1, 0, lambda c: bnd[:, 0, c:c + 1], scr)
        norm_cols(NLAT - 1, 1, 1, 0, lambda c: bnd[:, 1, c:c + 1], scr)
        snd2_v = snd2[:, :].rearrange("s (p c) -> s p c", p=128)
        rcv2_v = rcv2[:, :].rearrange("r (p c) -> r p c", p=128)
        P.dma(V(snd2_v[0], [("snd2", 0)]), bnd[:, 0, :], "bnd")
        P.dma(V(snd2_v[1], [("snd2", 0)]), bnd[:, 1, :], "bnd")
        P.collective(V(snd2[:, :], [("snd2", 0)]), V(rcv2[:, :], [("rcv2", 0)]), "cc")
        P.dma(halor[:, 0, :], V(rcv2_v[1], [("rcv2", 0)]), "hal")
        P.dma(halor[:, 1, :], V(rcv2_v[2], [("rcv2", 0)]), "hal")
        for s_ in range(2):
            P.ts("dve", halo[:, :, s_], halor[:, s_, :], hval[:, s_:s_ + 1], None, ALU.mult)

        aps = Ring([Buf(ps_t, 512 * b, [512], F32, "ps") for b in (1, 2)])
        gps = Ring([Buf(ps_t, 512 * b, [512], F32, "ps") for b in (3, 4)])
        yps = Ring([Buf(ps_t, 512 * b, [512], F32, "ps") for b in (5, 6)])
        eps2 = Buf(ps_t, 512 * 7, [512], F32, "ps")

        segs = ([] if last else [(2048, 256, 1, None, None)]) + [(0, 1024, 0, 0, None), (1024, 1024, 0, None, 1)]
        for (s0, n, col, lh, rh) in segs:
            lo = s0
            hi = s0 + n + 1 if (rh is None and col == 0) else s0 + n
            t = lo
            while t < hi:
                m = min(512, hi - t)
                o0 = t - (s0 - 1)
                norm_cols(t, m, 1, col, lambda c, o0=o0, m=m: h2[:, c, o0:o0 + m], scr)
                t += m
            if col == 1:
                P.memset("dve", h2[:, :, 0:1], 0.0)
                P.memset("dve", h2[:, :, n + 1:n + 2], 0.0)
            if lh is not None:
                P.copy("dve", h2[:, :, 0], halo[:, :, 0])
                P.copy("dve", h2sv[:, :, 0], h2[:, :, 1024])
            elif col == 0:
                P.copy("dve", h2[:, :, 0], h2sv[:, :, 0])
            if rh is not None:
                P.copy("dve", h2[:, :, n + 1], halo[:, :, 1])
            pieces = [(0, 512), (512, 1024), (1024, 1026)] if n == 1024 else [(0, 258)]
            pend_f = []
            for f in range(NF):
                wu = wur.next()
                P.dma(wu.all(), V(wup_v[f], []), f"wu{(wur.i - 1) % 2}", eng="pool")
                ua = uar.next()
                ug = ugr.next()
                for (p0, p1) in pieces:
                    m = p1 - p0
                    if m > 2:
                        pa, pg = aps.next(), gps.next()
                        pav, pgv = pa[:, 0:m], pg[:, 0:m]
                    else:
                        pav, pgv = eps2[:, 0:m], eps2[:, 8:8 + m]
                    for k in range(8):
                        P.mm(pav, wu[:, 0, k, :], h2[:, k, p0:p1], start=(k == 0), stop=(k == 7))
                    for k in range(8):
                        P.mm(pgv, wu[:, 1, k, :], h2[:, k, p0:p1], start=(k == 0), stop=(k == 7))
                    P.copy("act", ua[:, p0:p1], pav)
                    P.copy("act", ug[:, p0:p1], pgv)
                ta = tar.next()[:, 0:n]
                P.ts("dve", ta, ua[:, 0:n], CW[:, f, 0:1], CW[:, f, 3:4], ALU.mult, ALU.add)
                P.stt(ta, ua[:, 1:n + 1], CW[:, f, 1:2], ta, ALU.mult, ALU.add)
                P.stt(ta, ua[:, 2:n + 2], CW[:, f, 2:3], ta, ALU.mult, ALU.add)
                tg = tgr.next()[:, 0:n]
                fg = NF + f
                P.ts("dve", tg, ug[:, 0:n], CW[:, fg, 0:1], CW[:, fg, 3:4], ALU.mult, ALU.add)
                P.stt(tg, ug[:, 1:n + 1], CW[:, fg, 1:2], tg, ALU.mult, ALU.add)
                P.stt(tg, ug[:, 2:n + 2], CW[:, fg, 2:3], tg, ALU.mult, ALU.add)
                if pend_f:
                    ta_, tg_, f_ = pend_f.pop()
                    P.act(tg_, tg_, AF.Silu)
                    P.tt("dve", actT[:, f_, 0:n], ta_, tg_, ALU.mult)
                pend_f.append((ta, tg, f))
            while pend_f:
                ta_, tg_, f_ = pend_f.pop()
                P.act(tg_, tg_, AF.Silu)
                P.tt("dve", actT[:, f_, 0:n], ta_, tg_, ALU.mult)
            for dc in range(8):
                wd = wdr.next()
                P.dma(wd.all(), V(wdn_v[dc], []), f"wd{(wdr.i - 1) % 2}", eng="pool")
                for p0 in range(0, n, 512):
                    m = min(512, n - p0)
                    yp = yps.next()
                    for f in range(NF):
                        P.mm(yp[:, 0:m], wd[:, f, :], actT[:, f, p0:p0 + m], start=(f == 0), stop=(f == NF - 1))
                    xs = xT[:, dc, s0 + p0:s0 + p0 + m]
                    P.stt(xs, yp[:, 0:m], modb[:, 40 + dc, col:col + 1], xs, ALU.mult, ALU.add)

    for i in layer_ids:
        emit_mod(i)
        emit_attention(i)
        if stage == "full":
            emit_ffn(i)
    P.dma(V(yout[:, :].rearrange("p (c t) -> p c t", c=8), [("yout", 0)]), xT.all(), "sty")
    fin = P.lastw[("yout", 0)]
    P.emit([fin])
    return nc, es, P


GA_COS_OFF = NLAT + 512


def _chunk_rows(w):
    K, N = w.shape
    return np.ascontiguousarray(w.reshape(K // 128, 128, N).transpose(1, 0, 2).reshape(128, -1))


def _rope_tables():
    t = np.arange(4096)
    row = (t // 64).astype(np.float32)
    col = (t % 64).astype(np.float32)
    inv = (np.float32(10000.0) ** (-np.arange(16, dtype=np.float32) / np.float32(16))).astype(np.float32)
    ang = np.concatenate([row[:, None] * inv, col[:, None] * inv], axis=-1).astype(np.float32)
    cos = np.cos(ang).astype(np.float32)
    sin = np.sin(ang).astype(np.float32)
    idx = np.arange(128) % 32
    cosT = np.ascontiguousarray(cos[:, idx].T)
    sinT = np.ascontiguousarray(sin[:, idx].T)
    return cosT, sinT


def _na_tables(rpb):
    kp = np.arange(128) // 64
    kc = np.arange(128) % 64
    cs = np.clip(kc - 8, 0, 48)
    colok = (kc[:, None] >= cs[None, :]) & (kc[:, None] < cs[None, :] + 16)
    dcidx = np.clip(kc[:, None] - kc[None, :], -15, 15) + 15

    def tab(o):
        dr = 2 * o + kp[:, None] - kp[None, :]
        ok = colok & (np.abs(dr) <= 7)
        dridx = np.clip(dr, -7, 7) + 7
        t = rpb[:, dridx, dcidx]
        return np.where(ok[None], t, np.float32(NEG)).astype(np.float32)
    btab = np.zeros((16, 128, 7, 128), np.float32)
    for s, o in enumerate([-2, -1, 0, 1, 2]):
        btab[:, :, s, :] = tab(o)
    bx = np.stack([tab(3), tab(-3)], axis=2)
    return btab.reshape(16 * 128, 7 * 128), np.ascontiguousarray(bx).reshape(16 * 128, 2 * 128)


def _na_mrow(p):
    M = np.zeros((2, 16, 8, 2), np.float32)
    for lm in range(16):
        m = lm + 16 * p
        for s in range(8):
            if s < 5:
                o = s - 2
            elif s in (5, 6):
                continue
            else:
                o = 3 if lm == 0 else (-3 if lm == 15 else None)
                if o is None:
                    continue
            for kp in range(2):
                for qp in range(2):
                    kr = 2 * (m + o) + kp
                    qr = 2 * m + qp
                    rs = min(max(qr - 4, 0), 56)
                    ok = (0 <= kr <= 63) and (rs <= kr <= rs + 7)
                    M[kp, lm, s, qp] = 0.0 if ok else NEGM
    return M.reshape(2, 256)


def _swa_tables(p):
    a = np.arange(128)[:, None]
    b = np.arange(128)[None, :]
    prev = np.where(a >= b, 0.0, NEG).astype(np.float32)
    nxt = np.where(a <= b, 0.0, NEG).astype(np.float32)
    allneg = np.full((128, 128), NEG, np.float32)
    var = [prev, nxt, allneg if p == 0 else prev, nxt if p == 0 else allneg]
    t = np.stack([np.tile(v, (1, 4)) for v in var], axis=1)
    return np.ascontiguousarray(t).reshape(128, 4 * 512)


_PROG_CACHE = {}


def _get_prog(layer_ids):
    key = tuple(layer_ids)
    if key not in _PROG_CACHE:
        _PROG_CACHE[key] = build(list(layer_ids))
    return _PROG_CACHE[key][0]


def _layer_inputs(i, inp):
    kind, j = i % 3, i // 3
    f32 = np.float32
    d = {}
    d[f"wmod{i}"] = _chunk_rows(inp["w_mod"][i])
    d[f"bmod{i}"] = np.ascontiguousarray(inp["b_mod"][i].reshape(48, 128).T)
    d[f"gat{i}"] = np.ascontiguousarray(inp["g_attn"][i].reshape(8, 128).T)
    d[f"gff{i}"] = np.ascontiguousarray(inp["g_ffn"][i].reshape(8, 128).T)
    if kind == 0:
        w, wo, gq, gk = inp["na_w_qkv"][j], inp["na_w_o"][j], inp["na_g_q"][j], inp["na_g_k"][j]
        wq, wk, wv = w[:, :1024], w[:, 1024:2048], w[:, 2048:]
    else:
        pre = "swa" if kind == 1 else "ga"
        w, wo, gq, gk = inp[pre + "_w_qkv"][j], inp[pre + "_w_o"][j], inp[pre + "_g_q"][j], inp[pre + "_g_k"][j]
        wq0, wk, wv = w[:, :1024], w[:, 1024:1280], w[:, 1280:]
        cols = []
        for cq in range(8):
            for hf in range(2):
                g = 2 * (cq // 4) + hf
                h = 4 * g + cq % 4
                cols.extend(range(h * 64, h * 64 + 64))
        wq = wq0[:, cols]
    d[f"wq{i}"] = _chunk_rows(wq)
    d[f"wk{i}"] = _chunk_rows(wk)
    d[f"wv{i}"] = _chunk_rows(wv)
    d[f"wo{i}"] = _chunk_rows(wo)
    d[f"gqk{i}"] = np.ascontiguousarray(np.stack([np.tile(gq, 2), np.tile(gk, 2)], axis=1).astype(f32))
    if kind == 0:
        bt, bx = _na_tables(inp["na_rpb"][j])
        d[f"btab{i}"] = bt
        d[f"bx{i}"] = bx
    if kind == 1:
        d[f"sink{i}"] = np.ascontiguousarray(inp["swa_sink"][j].reshape(1, 16))
    wup = inp["ffn_w_up"][i]
    d[f"wup{i}"] = np.ascontiguousarray(
        wup.reshape(8, 128, 2, NF, 128).transpose(3, 1, 2, 0, 4).reshape(NF * 128, 2 * 8 * 128))
    cw = np.concatenate([inp["ffn_conv_w"][i], inp["ffn_conv_b"][i][None]], axis=0)
    d[f"cw{i}"] = np.ascontiguousarray(cw.reshape(4, 44, 128).transpose(2, 1, 0).reshape(128, 44 * 4))
    wdn = inp["ffn_w_down"][i]
    d[f"wdn{i}"] = np.ascontiguousarray(
        wdn.reshape(NF, 128, 8, 128).transpose(2, 1, 0, 3).reshape(8 * 128, NF * 128))
    return d


def _core_inputs(r, layer_ids, inp, shared, xin_r):
    b, p = r // 2, r % 2
    m = {"xin": xin_r}
    m["cin"] = np.ascontiguousarray(
        np.stack([inp["c"][b].reshape(8, 128).T, inp["c_ctx"].reshape(8, 128).T], axis=2).reshape(128, 16))
    m["cmat"] = shared["cmat"]
    m["e2"] = shared["e2"]
    m["hval"] = np.ascontiguousarray(np.tile(np.array([[float(p == 1), float(p == 0)]], np.float32), (128, 1)))
    cosT, sinT = shared["rope"]
    own = slice(p * 2048, p * 2048 + 2048)
    pre0 = max(p * 2048 - 256, 0)
    post0 = min(p * 2048 + 2048, 4096 - 256)
    m["cosd"] = np.ascontiguousarray(np.concatenate([cosT[:, own], cosT[:, pre0:pre0 + 256], cosT[:, post0:post0 + 256], cosT], axis=1))
    m["sind"] = np.ascontiguousarray(np.concatenate([sinT[:, own], sinT[:, pre0:pre0 + 256], sinT[:, post0:post0 + 256], sinT], axis=1))
    for i in layer_ids:
        m.update(shared["layers"][i])
        if i % 3 == 0:
            m[f"mrow{i}"] = shared["mrow"][p]
        if i % 3 == 1:
            m[f"swab{i}"] = shared["swab"][p]
    return m


def _shared(layer_ids, inp):
    sh = {}
    ones = np.ones((128, 128), np.float32)
    bd = np.zeros((128, 128), np.float32)
    bd[:64, :64] = 1.0
    bd[64:, 64:] = 1.0
    rot = np.zeros((128, 128), np.float32)
    for m_ in range(128):
        i = m_ % 64
        if i < 32:
            rot[m_ + 32, m_] = -1.0
        else:
            rot[m_ - 32, m_] = 1.0
    sh["cmat"] = np.ascontiguousarray(np.stack([ones, bd, rot], axis=1).reshape(128, 384))
    e2 = np.zeros((2, 128), np.float32)
    e2[0, :64] = 1.0
    e2[1, 64:] = 1.0
    sh["e2"] = e2
    sh["rope"] = _rope_tables()
    sh["layers"] = {i: _layer_inputs(i, inp) for i in layer_ids}
    sh["mrow"] = [_na_mrow(0), _na_mrow(1)]
    sh["swab"] = [_swa_tables(0), _swa_tables(1)]
    return sh


def _x_layout(inp):
    xs = []
    for r in range(8):
        b, p = r // 2, r % 2
        xa = np.concatenate([inp["x"][b, p * 2048:(p + 1) * 2048], inp["ctx"][b]], axis=0)
        xs.append(np.ascontiguousarray(xa.T.reshape(8, 128, NTOK).transpose(1, 0, 2).reshape(128, 8 * NTOK)))
    return xs


LAUNCH_PLAN = [[0, 1, 2, 3]]


def kernel(**inputs):
    inp = {k: np.asarray(v, dtype=np.float32) for k, v in inputs.items()}
    xs = _x_layout(inp)
    for layer_ids in LAUNCH_PLAN:
        nc = _get_prog(layer_ids)
        sh = _shared(layer_ids, inp)
        in_maps = [_core_inputs(r, layer_ids, inp, sh, xs[r]) for r in range(8)]
        res = run_bass_kernel_spmd(nc, in_maps, core_ids=list(range(8)))
        xs = [np.asarray(res.results[r]["yout"]) for r in range(8)]
    out = np.empty((4, 4096, 1024), np.float32)
    for r in range(8):
        b, p = r // 2, r % 2
        y = xs[r].reshape(128, 8, NTOK)[:, :, :NLAT]
        out[b, p * 2048:(p + 1) * 2048] = y.transpose(2, 1, 0).reshape(NLAT, 1024)
    return out
```
